# Optimizing a Trainium2 kernel written in Bass

```python
import math
import jax, jax.numpy as jnp
from jax import lax
import numpy as np

D_MODEL = 1024
BATCH = 8
SEQ = 8192
DEPTH = 1
DEC_BATCH = 32
DEC_SEQ = 2048
PAST_LEN = 128

GDN_HEADS = 4
GDN_DK = 128
GDN_DV = 128
CONV_WIDTH = 5
CONV_PAD = CONV_WIDTH // 2
CHUNK = 64
DIFF_HEADS = 4
DIFF_DQK = 64
DIFF_DV = 2 * DIFF_DQK
ROPE_THETA = 500000.0
ROPE_DIM = DIFF_DQK // 4
Q_BLOCK = 128
D_FF = int(math.ceil(8 * D_MODEL / 3 / 256)) * 256
ALPHA = (2 * DEPTH) ** 0.25
INIT_BETA = (8 * DEPTH) ** -0.25
GDN_CONV_CH = 2 * GDN_HEADS * GDN_DK + GDN_HEADS * GDN_DV
GDN_Z = GDN_HEADS * GDN_DV
GDN_GATES = 4 * GDN_HEADS
DIFF_QK = DIFF_HEADS * 2 * DIFF_DQK
DIFF_VW = DIFF_HEADS * DIFF_DV
IN_COLS = GDN_CONV_CH + GDN_Z + GDN_GATES + 2 * DIFF_QK + DIFF_VW
MIX_WIDTH = GDN_HEADS * GDN_DV + DIFF_HEADS * DIFF_DV

kernel_name = "hymba_gdn_diffattn_deepnorm_encoder"


def lambda_init_fn(layer):
    return 0.8 - 0.6 * math.exp(-0.3 * layer)


def layer_norm(x, g, b, eps=1e-5):
    xf = x.astype(jnp.float32)
    mu = jnp.mean(xf, -1, keepdims=True)
    var = jnp.mean(jnp.square(xf - mu), -1, keepdims=True)
    return ((xf - mu) * lax.rsqrt(var + eps) * g.astype(jnp.float32) + b.astype(jnp.float32)).astype(x.dtype)


def rms_norm(x, g, eps=1e-6):
    xf = x.astype(jnp.float32)
    return xf * lax.rsqrt(jnp.mean(xf * xf, -1, keepdims=True) + eps) * g.astype(jnp.float32)


def l2norm(t, eps=1e-6):
    return t * lax.rsqrt(jnp.sum(t * t, -1, keepdims=True) + eps)


def centred_depthwise_conv(x, w):
    return lax.conv_general_dilated(
        x, w[:, None, :].astype(x.dtype), window_strides=(1,), padding=[(CONV_PAD, CONV_PAD)],
        dimension_numbers=("NWC", "WIO", "NWC"), feature_group_count=x.shape[-1])


def delta_rule_chunked(q, k, v, g, beta):
    B, S, H, DK = q.shape
    DV = v.shape[-1]
    N = S // CHUNK
    ch = lambda t: jnp.moveaxis(t.reshape((B, N, CHUNK, H) + t.shape[3:]), 3, 1)
    q, k, v, g, beta = ch(q), ch(k), ch(v), ch(g), ch(beta)
    gc = jnp.cumsum(g, axis=-1)
    incl = jnp.tril(jnp.ones((CHUNK, CHUNK), bool))
    strict = jnp.tril(jnp.ones((CHUNK, CHUNK), bool), -1)
    decay = jnp.exp(jnp.where(incl, gc[..., :, None] - gc[..., None, :], -jnp.inf))
    kb = k * beta[..., None]
    m = jnp.where(strict, jnp.einsum('bhnid,bhnjd->bhnij', kb, k) * decay, 0.0)
    a_mat = m + jnp.eye(CHUNK, dtype=jnp.float32)
    rhs = jnp.concatenate([v * beta[..., None], kb * jnp.exp(gc)[..., None]], -1)
    sol = lax.linalg.triangular_solve(a_mat, rhs, left_side=True, lower=True, unit_diagonal=True)
    u, w = sol[..., :DV], sol[..., DV:]
    attn = jnp.einsum('bhnid,bhnjd->bhnij', q, k) * decay
    qd = q * jnp.exp(gc)[..., None]
    kd = k * jnp.exp(gc[..., -1:] - gc)[..., None]
    glast = jnp.exp(gc[..., -1])

    def step(state, xs):
        u_c, w_c, attn_c, qd_c, kd_c, gl_c = xs
        v_new = u_c - jnp.einsum('bhcd,bhde->bhce', w_c, state)
        o = jnp.einsum('bhcd,bhde->bhce', qd_c, state) + jnp.einsum('bhij,bhje->bhie', attn_c, v_new)
        state = state * gl_c[..., None, None] + jnp.einsum('bhcd,bhce->bhde', kd_c, v_new)
        return state, o

    xs = tuple(jnp.moveaxis(t, 2, 0) for t in (u, w, attn, qd, kd, glast))
    s0 = jnp.zeros((B, H, DK, DV), jnp.float32)
    _, o = lax.scan(step, s0, xs)
    return jnp.transpose(o, (1, 0, 3, 2, 4)).reshape(B, S, H, DV)


def gated_deltanet_group(qkv, z, gates, conv_w, a_log, dt_bias, norm_g):
    B, S, _ = qkv.shape
    c = jax.nn.silu(centred_depthwise_conv(qkv, conv_w)).astype(jnp.float32)
    q, k, v = jnp.split(c, [GDN_HEADS * GDN_DK, 2 * GDN_HEADS * GDN_DK], axis=-1)
    q = l2norm(q.reshape(B, S, GDN_HEADS, GDN_DK)) * (GDN_DK ** -0.5)
    k = l2norm(k.reshape(B, S, GDN_HEADS, GDN_DK))
    v = v.reshape(B, S, GDN_HEADS, GDN_DV)
    gt = gates.astype(jnp.float32).reshape(B, S, 4, GDN_HEADS)
    beta = jax.nn.sigmoid(gt[:, :, 0:2])
    g = -jnp.exp(a_log.astype(jnp.float32)) * jax.nn.softplus(gt[:, :, 2:4] + dt_bias.astype(jnp.float32))
    o_f = delta_rule_chunked(q, k, v, g[:, :, 0], beta[:, :, 0])
    flip = lambda t: jnp.flip(t, axis=1)
    o_b = flip(delta_rule_chunked(flip(q), flip(k), flip(v), flip(g[:, :, 1]), flip(beta[:, :, 1])))
    o = rms_norm(o_f + o_b, norm_g) * jax.nn.silu(z.astype(jnp.float32).reshape(B, S, GDN_HEADS, GDN_DV))
    return o.reshape(B, S, GDN_HEADS * GDN_DV)


def partial_rope(x, cos, sin):
    half = ROPE_DIM // 2
    x = x.astype(jnp.float32)
    c = cos[:, None, None, :]
    s = sin[:, None, None, :]
    x1, x2, rest = x[..., :half], x[..., half:ROPE_DIM], x[..., ROPE_DIM:]
    return jnp.concatenate([x1 * c - x2 * s, x2 * c + x1 * s, rest], -1)


def diff_attention_group(dq, dk, dv, lam_qk, norm_g, lam_init):
    B, S, _ = dq.shape
    inv = 1.0 / (ROPE_THETA ** (jnp.arange(0, ROPE_DIM, 2, dtype=jnp.float32) / ROPE_DIM))
    ang = jnp.arange(S, dtype=jnp.float32)[:, None] * inv[None, :]
    cos, sin = jnp.cos(ang), jnp.sin(ang)
    q = partial_rope(dq.reshape(B, S, DIFF_HEADS, 2, DIFF_DQK), cos, sin) * (DIFF_DQK ** -0.5)
    k = partial_rope(dk.reshape(B, S, DIFF_HEADS, 2, DIFF_DQK), cos, sin)
    v = dv.astype(jnp.float32).reshape(B, S, DIFF_HEADS, DIFF_DV)
    lq = lam_qk.astype(jnp.float32)
    lam = jnp.exp(jnp.sum(lq[0] * lq[1])) - jnp.exp(jnp.sum(lq[2] * lq[3])) + lam_init
    nb = S // Q_BLOCK
    qb = jnp.moveaxis(q.reshape(B, nb, Q_BLOCK, DIFF_HEADS, 2, DIFF_DQK), 1, 0)

    def block(qi):
        p = jax.nn.softmax(jnp.einsum('bqhcd,bkhcd->bhcqk', qi, k), axis=-1)
        wts = p[:, :, 0] - lam * p[:, :, 1]
        return jnp.einsum('bhqk,bkhd->bqhd', wts, v)

    o = jnp.moveaxis(lax.map(block, qb), 0, 1).reshape(B, S, DIFF_HEADS, DIFF_DV)
    o = rms_norm(o, norm_g) * (1.0 - lam_init)
    return o.reshape(B, S, DIFF_HEADS * DIFF_DV)


def hybrid_mixer(x, w_in, conv_w, a_log, dt_bias, gdn_norm_g, lam_qk, diff_norm_g, w_out, lam_init):
    h = x @ w_in
    splits = np.cumsum([GDN_CONV_CH, GDN_Z, GDN_GATES, DIFF_QK, DIFF_QK]).tolist()
    qkv, z, gates, dq, dk, dv = jnp.split(h, splits, axis=-1)
    o_a = gated_deltanet_group(qkv, z, gates, conv_w, a_log, dt_bias, gdn_norm_g)
    o_b = diff_attention_group(dq, dk, dv, lam_qk, diff_norm_g, lam_init)
    o = jnp.concatenate([o_a, o_b], -1).astype(x.dtype)
    return o @ w_out


def swiglu(x, w_gate_up, w_down):
    gate, up = jnp.split(x @ w_gate_up, 2, axis=-1)
    return (jax.nn.silu(gate) * up) @ w_down


def setup_inputs(seed: int = 0) -> dict:
    key = jax.random.key(seed)
    ks = jax.random.split(key, 20)
    f32 = jnp.float32
    nrm = lambda k, shape, s: jax.random.normal(k, shape, f32) * s
    dt = jnp.exp(jax.random.uniform(ks[5], (DEPTH, 2, GDN_HEADS), f32, math.log(1e-3), math.log(1e-1)))
    return {
        "x_prompt": jax.random.normal(ks[0], (BATCH, SEQ, D_MODEL), f32),
        "x_sample": jax.random.normal(ks[1], (DEC_BATCH, DEC_SEQ, D_MODEL), f32),
        "w_in": nrm(ks[2], (DEPTH, D_MODEL, IN_COLS), D_MODEL ** -0.5),
        "conv_w": nrm(ks[3], (DEPTH, CONV_WIDTH, GDN_CONV_CH), CONV_WIDTH ** -0.5),
        "a_log": jnp.log(jax.random.uniform(ks[4], (DEPTH, 2, GDN_HEADS), f32, 1.0, 16.0)),
        "dt_bias": dt + jnp.log(-jnp.expm1(-dt)),
        "gdn_norm_g": 1.0 + nrm(ks[6], (DEPTH, GDN_DV), 0.02),
        "lam_qk": nrm(ks[7], (DEPTH, 4, DIFF_DQK), 0.1),
        "diff_norm_g": 1.0 + nrm(ks[8], (DEPTH, DIFF_DV), 0.02),
        "w_out": nrm(ks[9], (DEPTH, MIX_WIDTH, D_MODEL), MIX_WIDTH ** -0.5 * INIT_BETA),
        "ln1_g": 1.0 + nrm(ks[10], (DEPTH, D_MODEL), 0.02),
        "ln1_b": nrm(ks[11], (DEPTH, D_MODEL), 0.02),
        "w_gate_up": nrm(ks[12], (DEPTH, D_MODEL, 2 * D_FF), D_MODEL ** -0.5),
        "w_down": nrm(ks[13], (DEPTH, D_FF, D_MODEL), D_FF ** -0.5 * INIT_BETA),
        "ln2_g": 1.0 + nrm(ks[14], (DEPTH, D_MODEL), 0.02),
        "ln2_b": nrm(ks[15], (DEPTH, D_MODEL), 0.02),
    }


def reference(x_prompt, x_sample, w_in, conv_w, a_log, dt_bias, gdn_norm_g, lam_qk, diff_norm_g,
              w_out, ln1_g, ln1_b, w_gate_up, w_down, ln2_g, ln2_b):
    def trunk(x):
        for l in range(DEPTH):
            mix = hybrid_mixer(x, w_in[l], conv_w[l], a_log[l], dt_bias[l], gdn_norm_g[l], lam_qk[l],
                               diff_norm_g[l], w_out[l], lambda_init_fn(l))
            x = layer_norm(ALPHA * x + mix, ln1_g[l], ln1_b[l])
            x = layer_norm(ALPHA * x + swiglu(x, w_gate_up[l], w_down[l]), ln2_g[l], ln2_b[l])
        return x

    y_prompt = trunk(x_prompt)
    y_sample = trunk(x_sample)
    return (y_prompt, y_sample)
```

```python
import math
import numpy as np
import ml_dtypes
import concourse.bass as bass
import concourse.mybir as mybir
from concourse.bass_utils import run_bass_kernel_spmd

F32 = mybir.dt.float32
BF16 = mybir.dt.bfloat16
AF = mybir.ActivationFunctionType
ALU = mybir.AluOpType
AX = mybir.AxisListType

D_MODEL = 1024
IN_COLS = 3600
D_FF = 2816
C_Z, C_G, C_DQ, C_DK, C_DV = 1536, 2048, 2176, 2688, 3200
W_COLS = 3712
ROPE_THETA = 500000.0
ALPHA = 2.0 ** 0.25
LAM_INIT = 0.8 - 0.6 * math.exp(-0.3 * 0)
NEG = -30000.0

ENGS = ["pe", "act", "dve", "pool", "sp"]
PE_ = "pool"
KCUT = 9
ROPEQ = "sp"
KR = 9


class Buf:
    __slots__ = ("name", "w", "r", "sem", "excl")

    def __init__(self, name="", excl=False):
        self.name = name
        self.excl = excl
        self.w = None
        self.r = []
        self.sem = None


class Sched:
    def __init__(self, nc):
        self.nc = nc
        self.ops = {e: [] for e in ENGS}
        self.nsem = 0
        self.semcount = {}
        self.barrier_tokens = {}

    def _newsem(self):
        k = self.nsem
        self.nsem += 1
        self.semcount[k] = 0
        return k

    def op(self, eng, meth, kw, reads=(), writes=(), dma_owner=None):
        fn = (meth, kw)
        reads = [r.b if isinstance(r, Tl) else r for r in reads]
        writes = [r.b if isinstance(r, Tl) else r for r in writes]
        if isinstance(dma_owner, Tl):
            dma_owner = dma_owner.b
        xr = [b for b in reads if b.excl and b not in writes]
        if xr:
            reads = [b for b in reads if not b.excl]
            writes = list(writes) + xr
        idx = len(self.ops[eng])
        waits = set()
        bt = self.barrier_tokens.pop(eng, None)
        if bt:
            waits |= bt
        for b in reads:
            t = b.w
            if t is not None:
                if t[0] == "c" and t[1] == eng and eng == "pe":
                    continue
                waits.add(t)
        for b in writes:
            t = b.w
            if t is not None and not (t[0] == "c" and t[1] == eng and eng == "pe"):
                waits.add(t)
            for t in b.r:
                if not (t[0] == "c" and t[1] == eng and eng == "pe"):
                    waits.add(t)
        if dma_owner is not None:
            if dma_owner.sem is None:
                dma_owner.sem = self._newsem()
            k = dma_owner.sem
            self.semcount[k] += 16
            tok = ("d", k, self.semcount[k])
        else:
            tok = ("c", eng, idx)
        for b in reads:
            b.r.append(tok)
        for b in writes:
            b.w = tok
            b.r = []
        self.ops[eng].append([fn, waits, tok])
        return tok

    def mm(self, out, lhsT, rhs, start=True, stop=True, reads=(), writes=()):
        return self.op("pe", "matmul", dict(out=out, lhsT=lhsT, rhs=rhs, start=start, stop=stop, skip_group_check=True), reads, writes)

    def tr(self, out, in_, ident, reads=(), writes=()):
        return self.op("pe", "transpose", dict(out=out, in_=in_, identity=ident), reads, writes)

    def dma(self, q, out, in_, reads=(), writes=(), owner=None):
        return self.op(q, "dma_start", dict(out=out, in_=in_), reads, writes, dma_owner=owner)

    def act(self, out, in_, func, reads=(), writes=(), **kw):
        d = dict(out=out, in_=in_, func=func)
        d.update(kw)
        return self.op("act", "activation", d, reads, writes)

    def cp(self, eng, out, in_, reads=(), writes=()):
        if eng == "act":
            return self.op("act", "copy", dict(out=out, in_=in_), reads, writes)
        return self.op(eng, "tensor_copy", dict(out=out, in_=in_), reads, writes)

    def tt(self, eng, out, in0, in1, op, reads=(), writes=()):
        return self.op(eng, "tensor_tensor", dict(out=out, in0=in0, in1=in1, op=op), reads, writes)

    def ts(self, eng, out, in0, s1, s2, op0, op1=None, reads=(), writes=(), **kw):
        d = dict(out=out, in0=in0, scalar1=s1, scalar2=s2, op0=op0)
        if op1 is not None:
            d["op1"] = op1
        d.update(kw)
        return self.op(eng, "tensor_scalar", d, reads, writes)

    def stt(self, out, in0, scalar, in1, op0, op1, reads=(), writes=(), eng="dve"):
        return self.op(eng, "scalar_tensor_tensor", dict(out=out, in0=in0, scalar=scalar, in1=in1, op0=op0, op1=op1), reads, writes)

    def memset(self, eng, ap, val, writes=()):
        return self.op(eng, "memset", dict(ap=ap, constant=val), (), writes)

    def barrier(self):
        toks = set()
        for e in ENGS:
            n = len(self.ops[e])
            for i in range(n - 1, -1, -1):
                if self.ops[e][i][2][0] == "c":
                    toks.add(("c", e, i))
                    break
        for k, v in self.semcount.items():
            if v > 0:
                toks.add(("d", k, v))
        for e in ENGS:
            cur = self.barrier_tokens.get(e, set())
            self.barrier_tokens[e] = cur | {t for t in toks if not (t[0] == "c" and t[1] == e)}

    def emit(self):
        nc = self.nc
        self.barrier()
        final_waits = self.barrier_tokens.pop("sp", set())
        self.barrier_tokens = {}
        targets = {e: set() for e in ENGS}
        allw = [final_waits]
        for e in ENGS:
            for fn, waits, tok in self.ops[e]:
                allw.append(waits)
        for waits in allw:
            for t in waits:
                if t[0] == "c":
                    targets[t[1]].add(t[2])
        rank = {}
        for e in ENGS:
            rank[e] = {i: r + 1 for r, i in enumerate(sorted(targets[e]))}
        esem = {e: nc.alloc_semaphore(name=f"es_{e}") for e in ENGS}
        dsem = {k: nc.alloc_semaphore(name=f"ds_{k}") for k in range(self.nsem)}
        stats = {e: [len(self.ops[e]), 0] for e in ENGS}

        def run(e, engobj, extra_waits=None):
            seen = {}

            def do_waits(waits):
                best = {}
                for t in waits:
                    if t[0] == "c":
                        key = ("c", t[1])
                        val = rank[t[1]][t[2]]
                    else:
                        key = ("d", t[1])
                        val = t[2]
                    if val > best.get(key, 0):
                        best[key] = val
                for key, val in best.items():
                    if seen.get(key, 0) >= val:
                        continue
                    seen[key] = val
                    sem = esem[key[1]] if key[0] == "c" else dsem[key[1]]
                    engobj.wait_ge(sem, val)
                    stats[e][1] += 1

            for i, (fn, waits, tok) in enumerate(self.ops[e]):
                do_waits(waits)
                ins = getattr(engobj, fn[0])(**fn[1])
                if tok[0] == "d":
                    ins.then_inc(dsem[tok[1]], 16)
                elif i in rank[e]:
                    ins.then_inc(esem[e], 1)
            if extra_waits:
                do_waits(extra_waits)

        with nc.Block() as block:
            @block.tensor
            def _(eng):
                run("pe", eng)

            @block.scalar
            def _(eng):
                run("act", eng)

            @block.vector
            def _(eng):
                run("dve", eng)

            @block.gpsimd
            def _(eng):
                run("pool", eng)

            @block.sync
            def _(eng):
                run("sp", eng, final_waits)
        return stats


class Tl:
    __slots__ = ("t", "b")

    def __init__(self, t, name="", excl=False):
        self.t = t
        self.b = Buf(name, excl)


_DS = {F32: 4, BF16: 2}


class BankView:
    def __init__(self, t, off):
        self.t_, self.off = t, off

    def __getitem__(self, key):
        r, c = key
        a = (c.start or 0) + self.off
        b = (c.stop if c.stop is not None else 512) + self.off
        return self.t_[r, a:b]


class Arena:
    def __init__(self, nc, base=16512, limit=229344):
        self.nc = nc
        self.off = base
        self.limit = limit
        self.n = 0

    def alloc(self, name, shape, dtype):
        sz = int(np.prod(shape[1:])) * _DS[dtype]
        sz = (sz + 63) // 64 * 64
        assert self.off + sz <= self.limit, f"SBUF overflow at {name}: {self.off}+{sz}"
        self.n += 1
        t = self.nc.alloc_sbuf_tensor_at(f"{name}{self.n}", list(shape), dtype, offset=self.off)
        self.off += sz
        return Tl(t, name)

    def ring(self, name, n, shape, dtype):
        return Ring([self.alloc(f"{name}{i}_", shape, dtype) for i in range(n)])


class Ring:
    def __init__(self, items):
        self.items = items
        self.i = 0

    def next(self):
        it = self.items[self.i % len(self.items)]
        self.i += 1
        return it


def _consts(smax):
    c = {}
    c["ident_f"] = np.eye(128, dtype=np.float32)
    c["ident_b"] = np.eye(128, dtype=np.float32).astype(ml_dtypes.bfloat16)
    inv = (1.0 / (np.float32(ROPE_THETA) ** (np.arange(0, 16, 2, dtype=np.float32) / np.float32(16)))).astype(np.float32)
    ang = (np.arange(smax, dtype=np.float32)[:, None] * inv[None, :]).astype(np.float32)
    cos, sin = np.cos(ang).astype(np.float32), np.sin(ang).astype(np.float32)
    COS = np.ones((128, smax), np.float32)
    SIN = np.zeros((128, smax), np.float32)
    perm = np.zeros((128, 128), np.float32)
    for comp in range(2):
        b = comp * 64
        for p in range(8):
            COS[b + p] = cos[:, p]
            COS[b + 8 + p] = cos[:, p]
            SIN[b + p] = -sin[:, p]
            SIN[b + 8 + p] = sin[:, p]
            perm[b + 8 + p, b + p] = 1.0
            perm[b + p, b + 8 + p] = 1.0
    c["rope"] = np.stack([COS * 0.125, SIN * 0.125, COS, SIN], 0).astype(np.float32)
    c["perm"] = perm.astype(ml_dtypes.bfloat16)
    t = np.arange(128)
    same = (t[:, None] // 64) == (t[None, :] // 64)
    le = t[:, None] <= t[None, :]
    ge = t[:, None] >= t[None, :]
    lt = t[:, None] < t[None, :]
    gt = t[:, None] > t[None, :]
    gm = {}
    for d, (ma, mb, strict_ji, incl_ji, mac) in enumerate([
        (le & same, gt & same, lt & same, le & same, gt & same),
        (ge & same, lt & same, gt & same, ge & same, lt & same),
    ]):
        gm[f"ma2_{d}"] = np.concatenate([ma, ma], 1).astype(np.float32)
        gm[f"mb_{d}"] = mb.astype(np.float32)
        gm[f"neg2_{d}"] = np.concatenate([np.where(strict_ji, 0.0, NEG), np.where(incl_ji, 0.0, NEG)], 1).astype(ml_dtypes.bfloat16)
        gm[f"mac_{d}"] = mac.astype(np.float32)
    c["gm_f"] = np.stack([gm["ma2_0"], gm["ma2_1"]], 0)
    c["gm_mb"] = np.stack([gm["mb_0"], gm["mb_1"], gm["mac_0"], gm["mac_1"]], 0)
    c["gm_neg"] = np.stack([gm["neg2_0"], gm["neg2_1"]], 0)
    return c


class K:
    pass


def build(seqs, debug=False, phases="ABCD"):
    T = sum(seqs)
    smax = max(seqs)
    nc = bass.Bass("TRN2", target_bir_lowering=False)
    k = K()
    k.nc, k.T, k.seqs, k.smax = nc, T, seqs, smax
    k.S = Sched(nc)

    def din(name, shape, dt=F32):
        return nc.dram_tensor(name, list(shape), dt, kind="ExternalInput").ap()

    k.x = din("x", [T, D_MODEL])
    k.w_in = din("w_in", [D_MODEL, IN_COLS])
    k.conv_w = din("conv_w", [5, 1536])
    k.a_log = din("a_log", [1, 8])
    k.dt_bias = din("dt_bias", [1, 8])
    k.gdn_g = din("gdn_norm_g", [1, 128])
    k.lam_qk = din("lam_qk", [1, 256])
    k.diff_g = din("diff_norm_g", [1, 128])
    k.w_out = din("w_out", [1024, 1024])
    k.ln1_g = din("ln1_g", [1, 1024])
    k.ln1_b = din("ln1_b", [1, 1024])
    k.w_gu = din("w_gate_up", [1024, 2 * D_FF])
    k.w_down = din("w_down", [D_FF, 1024])
    k.ln2_g = din("ln2_g", [1, 1024])
    k.ln2_b = din("ln2_b", [1, 1024])
    k.c_ident_f = din("c_ident_f", [128, 128])
    k.c_ident_b = din("c_ident_b", [128, 128], BF16)
    k.c_rope = din("c_rope", [4, 128, smax])
    k.c_perm = din("c_perm", [128, 128], BF16)
    k.c_gm_f = din("c_gm_f", [2, 128, 256])
    k.c_gm_mb = din("c_gm_mb", [4, 128, 128])
    k.c_gm_neg = din("c_gm_neg", [2, 128, 256], BF16)

    skind = "ExternalOutput" if debug else "Internal"

    def dscr(name, shape, dt):
        return nc.dram_tensor(name, list(shape), dt, kind=skind).ap()

    k.qkvT = dscr("s_qkvT", [1536, T], BF16)
    k.zs = dscr("s_z", [T, 512], F32)
    k.gs = dscr("s_g", [T, 16], F32)
    k.dqT = dscr("s_dqT", [512, T], BF16)
    k.dkT = dscr("s_dkT", [512, T], BF16)
    k.dvs = dscr("s_dv", [T, 512], BF16)
    k.mixT = dscr("s_mixT", [1024, T], BF16)
    k.y = nc.dram_tensor("y", [T, D_MODEL], F32, kind="ExternalOutput").ap()
    k.pspair = [nc.alloc_psum_tensor(f"pp{i}", [128, 1024], F32) for i in range(4)]
    k.psum = [Tl(BankView(k.pspair[i // 2], (i % 2) * 512), f"ps{i}", excl=True) for i in range(8)]

    if "A" in phases:
        phase_A(k)
        k.S.barrier()
    if "B" in phases:
        phase_B(k)
        k.S.barrier()
    if "C" in phases:
        phase_C(k)
        k.S.barrier()
    if "D" in phases:
        phase_D(k)
    stats = k.S.emit()
    k.stats = stats
    return nc, k


def phase_A(k):
    nc, S, T = k.nc, k.S, k.T
    ar = Arena(nc)
    w_bf = ar.alloc("w_in_bf", [128, 8, W_COLS], BF16)
    wst = ar.ring("wst", 2, [128, IN_COLS], F32)
    ident_f = ar.alloc("ident_f", [128, 128], F32)
    perm = ar.alloc("perm", [128, 128], BF16)
    xs_r = ar.ring("xs", 2, [128, 4, 1024], F32)
    xT_r = ar.ring("xT", 2, [128, 8, 512], BF16)
    sb_r = ar.ring("stb", 6, [128, 512], BF16)
    sf_r = ar.ring("stf", 4, [128, 512], F32)
    rope_r = ar.ring("rope", 2, [128, 4, 512], F32)
    zst_r = ar.ring("zst", 2, [128, 4, 512], F32)
    dvst_r = ar.ring("dvst", 2, [128, 4, 512], BF16)
    gst_r = ar.ring("gst", 2, [128, 4, 16], F32)
    tr_banks = Ring(k.psum[0:2])
    mm_banks = Ring(k.psum[2:8])

    S.dma("sp", ident_f.t[:], k.c_ident_f, writes=[ident_f], owner=ident_f)
    S.dma("sp", perm.t[:], k.c_perm, writes=[perm], owner=perm)
    w_view = k.w_in.rearrange("(kk p) c -> kk p c", p=128)
    for kk in range(8):
        st = wst.next()
        S.dma("sp", st.t[:], w_view[kk], writes=[st], owner=st)
        S.cp(["dve", PE_, "act"][kk % 3], w_bf.t[:, kk, 0:2064], st.t[:, 0:2064], reads=[st], writes=[w_bf])
        S.cp(["dve", PE_, "act"][(kk + 1) % 3], w_bf.t[:, kk, C_DQ:C_DQ + 1536], st.t[:, 2064:3600], reads=[st], writes=[w_bf])

    tiles = []
    s0 = 0
    for sl in k.seqs:
        for p0 in range(0, sl, 512):
            tiles.append((s0 + p0, p0))
        s0 += sl

    def load_x(i):
        t0, _ = tiles[i]
        xs = xs_r.next()
        S.dma("sp", xs.t[:], k.x[t0:t0 + 512, :].rearrange("(a p) d -> p a d", p=128), writes=[xs], owner=xs)
        return xs

    ev = [0]

    def evac(out_ap, in_ap, reads, writes):
        S.cp("act" if ev[0] % 2 == 0 else "dve", out_ap, in_ap, reads, writes)
        ev[0] += 1

    nxt = load_x(0)
    for i, (t0, pos0) in enumerate(tiles):
        xs = nxt
        if i + 1 < len(tiles):
            nxt = load_x(i + 1)
        rope = rope_r.next()
        S.dma(ROPEQ, rope.t[:], k.c_rope[:, :, pos0:pos0 + 512].rearrange("f p s -> p f s"), writes=[rope], owner=rope)
        xT = xT_r.next()
        for kk in range(8):
            bank = tr_banks.next()
            for a in range(4):
                S.tr(bank.t[:, a * 128:(a + 1) * 128], xs.t[:, a, kk * 128:(kk + 1) * 128], ident_f.t[:], reads=[xs, ident_f], writes=[bank])
            evac(xT.t[:, kk, :], bank.t[:, :], [bank], [xT])

        def proj_fm(col0):
            bank = mm_banks.next()
            for kk in range(8):
                S.mm(bank.t[:, :], w_bf.t[:, kk, col0:col0 + 128], xT.t[:, kk, :], start=(kk == 0), stop=(kk == 7), reads=[w_bf, xT], writes=[bank])
            return bank

        for c in range(12):
            bank = proj_fm(c * 128)
            st = sb_r.next()
            evac(st.t[:], bank.t[:, :], [bank], [st])
            S.dma("sp", k.qkvT[c * 128:(c + 1) * 128, t0:t0 + 512], st.t[:], reads=[st], owner=st)
        for qk in range(2 if KCUT >= 2 else 0):
            dst = k.dqT if qk == 0 else k.dkT
            cbase = C_DQ if qk == 0 else C_DK
            for h in range(4):
                bank = proj_fm(cbase + h * 128)
                rawb = sb_r.next()
                S.cp("act", rawb.t[:], bank.t[:, :], reads=[bank], writes=[rawb])
                t1 = sf_r.next()
                if KR >= 2:
                    S.tt("dve", t1.t[:], rope.t[:, 2 * qk, :], bank.t[:, :], ALU.mult, reads=[bank, rope], writes=[t1])
                bank2 = mm_banks.next()
                t2 = sf_r.next()
                if KR >= 3:
                    S.mm(bank2.t[:, :], perm.t[:], rawb.t[:], reads=[perm, rawb], writes=[bank2])
                    S.tt("dve", t2.t[:], bank2.t[:, :], rope.t[:, 2 * qk + 1, :], ALU.mult, reads=[bank2, rope], writes=[t2])
                ob = sb_r.next()
                if KR >= 4:
                    S.tt(PE_, ob.t[:], t1.t[:], t2.t[:], ALU.add, reads=[t1, t2], writes=[ob])
                else:
                    ob = rawb
                S.dma("sp", dst[h * 128:(h + 1) * 128, t0:t0 + 512], ob.t[:], reads=[ob], owner=ob)
        zst, dvst, gst = zst_r.next(), dvst_r.next(), gst_r.next()
        for a in range(4 if KCUT >= 3 else 0):
            for (col0, ncol, dstt) in ((C_Z, 512, zst), (C_DV, 512, dvst), (C_G, 16, gst))[:KCUT - 2]:
                bank = mm_banks.next()
                for kk in range(8):
                    S.mm(bank.t[:, 0:ncol], xT.t[:, kk, a * 128:(a + 1) * 128], w_bf.t[:, kk, col0:col0 + ncol], start=(kk == 0), stop=(kk == 7), reads=[w_bf, xT], writes=[bank])
                evac(dstt.t[:, a, :], bank.t[:, 0:ncol], [bank], [dstt])
        if KCUT >= 3:
            S.dma("sp", k.zs[t0:t0 + 512, :].rearrange("(a p) c -> p a c", p=128), zst.t[:], reads=[zst], owner=zst)
        if KCUT >= 4:
            S.dma("sp", k.dvs[t0:t0 + 512, :].rearrange("(a p) c -> p a c", p=128), dvst.t[:], reads=[dvst], owner=dvst)
        if KCUT >= 5:
            S.dma("sp", k.gs[t0:t0 + 512, :].rearrange("(a p) c -> p a c", p=128), gst.t[:], reads=[gst], owner=gst)


def seqs_of(k):
    return k.seqs


def phase_B(k):
    nc, S = k.nc, k.S
    ar = Arena(nc)
    smax = k.smax
    ntm = smax // 128
    ident_f = ar.alloc("ident_f", [128, 128], F32)
    ident_b = ar.alloc("ident_b", [128, 128], BF16)
    ones_f = ar.alloc("ones_f", [128, 128], F32)
    ones_b1 = ar.alloc("ones_b1", [128, 128], BF16)
    ones_b128 = ar.alloc("ones_b128", [128, 128], BF16)
    ma2 = [ar.alloc(f"ma2{d}", [128, 256], F32) for d in range(2)]
    mbm = [ar.alloc(f"mb{d}", [128, 128], F32) for d in range(2)]
    mac = [ar.alloc(f"mac{d}", [128, 128], F32) for d in range(2)]
    neg2 = [ar.alloc(f"neg{d}", [128, 256], BF16) for d in range(2)]
    convw = ar.alloc("convw", [128, 12, 5], F32)
    diag = ar.alloc("diag", [128, 60, 128], BF16)
    negA = ar.alloc("negA", [128, 8], F32)
    dtb = ar.alloc("dtb", [128, 8], F32)
    gg = ar.alloc("gg", [128, 128], F32)
    junk = ar.alloc("junk", [128, 128], F32)
    G = ar.alloc("G", [128, ntm, 16], F32)
    BT = ar.alloc("BT", [128, ntm, 8], F32)
    NBT = ar.alloc("NBT", [128, ntm, 8], F32)
    GA = ar.alloc("GA", [128, ntm, 8], F32)
    EGL = ar.alloc("EGL", [128, ntm * 8], F32)
    ssq = ar.alloc("ssq", [128, ntm], F32)
    rstd = ar.alloc("rstd", [128, ntm], F32)
    qT = ar.alloc("qT", [128, smax], BF16)
    kT = ar.alloc("kT", [128, smax], BF16)
    k_tok = ar.alloc("k_tok", [128, ntm * 128], BF16)
    v_tok = ar.alloc("v_tok", [128, ntm * 128], BF16)
    oacc = ar.alloc("oacc", [128, ntm * 128], F32)
    mark = ar.off
    raw_r = ar.ring("raw", 6, [128, 516], BF16)
    c_r = ar.ring("c", 8, [128, 512], F32)
    vb_r = ar.ring("vb", 3, [128, 512], BF16)
    sq_r = ar.ring("sq", 4, [128, 512], BF16)
    ln_r = ar.ring("ln", 5, [128, 512], F32)
    end1 = ar.off
    ar.off = mark
    NL, DEPTH = 3, 6
    lanes = []
    for l in range(NL):
        lanes.append(dict(mag=ar.alloc(f"mag{l}", [128, 256], F32), e3=ar.alloc(f"e3{l}", [128, 384], F32),
                          z=ar.ring(f"z{l}", 3, [128, 384], F32),
                          P1=k.psum[2 * l], P2=k.psum[2 * l + 1]))
    slots = [[dict(Rb=ar.alloc("Rb", [128, 128], BF16), AT=ar.alloc("AT", [128, 128], BF16), kgT=ar.alloc("kgT", [128, 128], BF16),
                   qdT=ar.alloc("qdT", [128, 128], BF16), kd=ar.alloc("kd", [128, 128], BF16), gl=ar.alloc("gl", [128, 2], F32))
              for _ in range(DEPTH)] for d in range(2)]
    rr_r = [ar.ring(f"rr{d}", 3, [128, 128], BF16) for d in range(2)]
    vn_r = [ar.ring(f"vn{d}", 3, [128, 128], BF16) for d in range(2)]
    Sf = [ar.alloc(f"Sf{d}", [128, 128], F32) for d in range(2)]
    Sb = [ar.alloc(f"Sb{d}", [128, 128], BF16) for d in range(2)]
    end2 = ar.off
    ar.off = mark
    z_r = ar.ring("Z", 2, [128, 16, 128], F32)
    on_r = ar.ring("on", 3, [128, 128], BF16)
    ob_r = ar.ring("ob", 2, [128, 512], BF16)
    ar.off = max(end1, end2, ar.off)
    bk = Ring(k.psum[0:6])
    pS = k.psum[6:8]

    S.dma("sp", ident_f.t[:], k.c_ident_f, writes=[ident_f], owner=ident_f)
    S.dma("sp", ident_b.t[:], k.c_ident_b, writes=[ident_b], owner=ident_b)
    for d in range(2):
        S.dma("sp", ma2[d].t[:], k.c_gm_f[d], writes=[ma2[d]], owner=ma2[d])
        S.dma("sp", mbm[d].t[:], k.c_gm_mb[d], writes=[mbm[d]], owner=mbm[d])
        S.dma("sp", mac[d].t[:], k.c_gm_mb[2 + d], writes=[mac[d]], owner=mac[d])
        S.dma("sp", neg2[d].t[:], k.c_gm_neg[d], writes=[neg2[d]], owner=neg2[d])
    S.memset("pool", ones_f.t[:], 1.0, writes=[ones_f])
    S.memset("pool", ones_b1.t[:], 1.0, writes=[ones_b1])
    S.memset("pool", ones_b128.t[:], 128.0, writes=[ones_b128])
    cw_v = k.conv_w.rearrange("j (c p) -> c p j", p=128)
    for c in range(12):
        S.op("sp", "dma_start", dict(out=convw.t[:, c, :], in_=cw_v[c], allow_slow_non_contiguous=True), (), [convw.b], dma_owner=convw.b)
    for c in range(12):
        for j in range(5):
            S.ts("dve", diag.t[:, c * 5 + j, :], ident_f.t[:], convw.t[:, c, j:j + 1], None, ALU.mult,
                 reads=[ident_f, convw], writes=[diag])
    S.dma("sp", negA.t[:], k.a_log[0:1, :].broadcast_to([128, 8]), writes=[negA], owner=negA)
    S.dma("sp", dtb.t[:], k.dt_bias[0:1, :].broadcast_to([128, 8]), writes=[dtb], owner=dtb)
    S.dma("sp", gg.t[:], k.gdn_g[0:1, :].broadcast_to([128, 128]), writes=[gg], owner=gg)
    S.act(negA.t[:], negA.t[:], AF.Exp, reads=[negA], writes=[negA])
    S.ts("dve", negA.t[:], negA.t[:], -1.0, None, ALU.mult, reads=[negA], writes=[negA])
    S.ts("dve", gg.t[:], gg.t[:], math.sqrt(128.0), None, ALU.mult, reads=[gg], writes=[gg])

    def bview(bank):
        return bank.t[:, :].bitcast(BF16)

    ev = [0]

    def evac(out_ap, in_ap, reads, writes):
        S.cp("act" if ev[0] % 2 == 0 else "dve", out_ap, in_ap, reads, writes)
        ev[0] += 1

    for (s0, sl) in seq_starts(k):
        nt = sl // 128
        for c0 in range(0, nt, 16):
            c1 = min(nt, c0 + 16)
            S.dma("sp", G.t[:, c0:c1, :], k.gs[s0 + c0 * 128:s0 + c1 * 128, :].rearrange("(n p) c -> p n c", p=128), writes=[G], owner=G)
        S.act(BT.t[:, 0:nt, :], G.t[:, 0:nt, 0:8], AF.Exp, reads=[G], writes=[BT], scale=-1.0)
        S.ts("dve", BT.t[:, 0:nt, :], BT.t[:, 0:nt, :], 1.0, None, ALU.add, reads=[BT], writes=[BT])
        S.op("dve", "reciprocal", dict(out=BT.t[:, 0:nt, :], in_=BT.t[:, 0:nt, :]), reads=[BT], writes=[BT])
        S.ts("dve", NBT.t[:, 0:nt, :], BT.t[:, 0:nt, :], -1.0, None, ALU.mult, reads=[BT], writes=[NBT])
        S.tt("dve", GA.t[:, 0:nt, :], G.t[:, 0:nt, 8:16], dtb.t[:, 0:8].unsqueeze(1).broadcast_to([128, nt, 8]), ALU.add, reads=[G, dtb], writes=[GA])
        S.act(GA.t[:, 0:nt, :], GA.t[:, 0:nt, :], AF.Exp, reads=[GA], writes=[GA])
        S.act(GA.t[:, 0:nt, :], GA.t[:, 0:nt, :], AF.Ln, reads=[GA], writes=[GA], bias=1.0)
        S.tt("dve", GA.t[:, 0:nt, :], GA.t[:, 0:nt, :], negA.t[:, 0:8].unsqueeze(1).broadcast_to([128, nt, 8]), ALU.mult, reads=[GA, negA], writes=[GA])
        bank = bk.next()
        for t in range(nt):
            for d in range(2):
                S.mm(bank.t[:, t * 8 + d * 4:t * 8 + d * 4 + 4], mac[d].t[:], GA.t[:, t, d * 4:(d + 1) * 4], reads=[mac[d], GA], writes=[bank])
        S.act(EGL.t[:, 0:nt * 8], bank.t[:, 0:nt * 8], AF.Exp, reads=[bank], writes=[EGL])

        for h in range(4):
            S.barrier()
            nblk = sl // 512

            def b1_gen(blk):
                t0 = s0 + blk * 512
                cs, banks_c = [], []
                for which in range(3):
                    cidx = which * 4 + h
                    raw = raw_r.next()
                    lo, hi = 0, 516
                    if blk == 0:
                        S.memset("pool", raw.t[:, 0:2], 0.0, writes=[raw])
                        lo = 2
                    if blk == nblk - 1:
                        S.memset("pool", raw.t[:, 514:516], 0.0, writes=[raw])
                        hi = 514
                    S.dma("sp", raw.t[:, lo:hi], k.qkvT[cidx * 128:(cidx + 1) * 128, t0 - 2 + lo:t0 - 2 + hi], writes=[raw], owner=raw)
                    bank = bk.next()
                    for j in range(5):
                        S.mm(bank.t[:, :], diag.t[:, cidx * 5 + j, :], raw.t[:, j:j + 512], start=(j == 0), stop=(j == 4), reads=[diag, raw], writes=[bank])
                    banks_c.append(bank)
                yield
                for which in range(3):
                    c = c_r.next()
                    S.act(c.t[:], banks_c[which].t[:, :], AF.Silu, reads=[banks_c[which]], writes=[c])
                    cs.append(c)
                yield
                vb = vb_r.next()
                S.cp("pool", vb.t[:], cs[2].t[:], reads=[cs[2]], writes=[vb])
                sqs = []
                for which in range(2):
                    sq = sq_r.next()
                    S.tt("pool", sq.t[:], cs[which].t[:], cs[which].t[:], ALU.mult, reads=[cs[which]], writes=[sq])
                    sqs.append(sq)
                yield
                bks = []
                for which in range(2):
                    bank = bk.next()
                    S.mm(bank.t[:, :], (ones_b128 if which == 0 else ones_b1).t[:], sqs[which].t[:], reads=[ones_b128, ones_b1, sqs[which]], writes=[bank])
                    bks.append(bank)
                yield
                rss = []
                for which in range(2):
                    ln = ln_r.next()
                    S.act(ln.t[:], bks[which].t[:, :], AF.Ln, reads=[bks[which]], writes=[ln], bias=(128.0e-6 if which == 0 else 1.0e-6))
                    rss.append(ln)
                yield
                for which in range(2):
                    S.act(rss[which].t[:], rss[which].t[:], AF.Exp, reads=[rss[which]], writes=[rss[which]], scale=-0.5)
                yield
                for which in range(2):
                    dst = qT if which == 0 else kT
                    S.tt("dve", dst.t[:, blk * 512:(blk + 1) * 512], cs[which].t[:], rss[which].t[:], ALU.mult, reads=[cs[which], rss[which]], writes=[dst])
                yield
                trb = k.psum[6 + blk % 2]
                tv = bview(trb)
                for a in range(4):
                    S.tr(tv[:, a * 128:(a + 1) * 128], kT.t[:, blk * 512 + a * 128:blk * 512 + (a + 1) * 128], ident_b.t[:], reads=[kT, ident_b], writes=[trb])
                for a in range(4):
                    S.tr(tv[:, 512 + a * 128:512 + (a + 1) * 128], vb.t[:, a * 128:(a + 1) * 128], ident_b.t[:], reads=[vb, ident_b], writes=[trb])
                yield
                evac(k_tok.t[:, blk * 512:(blk + 1) * 512], tv[:, 0:512], [trb], [k_tok])
                evac(v_tok.t[:, blk * 512:(blk + 1) * 512], tv[:, 512:1024], [trb], [v_tok])

            gens, nb = [], 0
            while nb < nblk or gens:
                while len(gens) < 2 and nb < nblk:
                    gens.append(b1_gen(nb))
                    nb += 1
                for g in list(gens):
                    try:
                        next(g)
                    except StopIteration:
                        gens.remove(g)

            S.barrier()
            for d in range(2):
                S.memset("dve", Sf[d].t[:], 0.0, writes=[Sf[d]])
                S.memset("pool", Sb[d].t[:], 0.0, writes=[Sb[d]])
            prep_done = set()
            oacc_written = set()
            scan_emitted = [0, 0]

            def prep_gen(i, d, ln):
                tile = i if d == 0 else nt - 1 - i
                gcol = d * 4 + h
                tc = slice(tile * 128, (tile + 1) * 128)
                sl_ = slots[d][i % DEPTH]
                mag, e3, P1, P2 = ln["mag"], ln["e3"], ln["P1"], ln["P2"]
                S.ts("dve", mag.t[:], ma2[d].t[:], GA.t[:, tile, gcol:gcol + 1], None, ALU.mult, reads=[ma2[d], GA], writes=[mag])
                S.mm(P1.t[:, 0:256], mbm[d].t[:], mag.t[:], start=True, stop=False, reads=[mbm[d], mag], writes=[P1])
                S.mm(P1.t[:, 0:256], ident_b.t[:], neg2[d].t[:], start=False, stop=True, reads=[ident_b, neg2[d]], writes=[P1])
                S.mm(P1.t[:, 256:384], ones_f.t[:], mag.t[:, 0:128], reads=[ones_f, mag], writes=[P1])
                S.mm(P2.t[:, 0:128], kT.t[:, tc], kT.t[:, tc], reads=[kT], writes=[P2])
                S.mm(P2.t[:, 128:256], kT.t[:, tc], qT.t[:, tc], reads=[kT, qT], writes=[P2])
                yield
                S.act(e3.t[:], P1.t[:, 0:384], AF.Exp, reads=[P1], writes=[e3])
                yield
                z = ln["z"].next()
                S.stt(z.t[:, 128:256], P2.t[:, 0:128], NBT.t[:, tile, gcol:gcol + 1], e3.t[:, 0:128], ALU.mult, ALU.mult, reads=[P2, NBT, e3], writes=[z])
                S.tt("dve", sl_["AT"].t[:], P2.t[:, 128:256], e3.t[:, 128:256], ALU.mult, reads=[P2, e3], writes=[sl_["AT"]])
                S.cp("pool", z.t[:, 256:384], ident_f.t[:], reads=[ident_f], writes=[z])
                yield
                S.tr(P1.t[:, 384:512], z.t[:, 128:256], ident_f.t[:], reads=[z, ident_f], writes=[P1])
                S.tt("pool", sl_["kgT"].t[:], kT.t[:, tc], e3.t[:, 256:384], ALU.mult, reads=[kT, e3], writes=[sl_["kgT"]])
                S.tt("pool", sl_["qdT"].t[:], qT.t[:, tc], e3.t[:, 256:384], ALU.mult, reads=[qT, e3], writes=[sl_["qdT"]])
                S.ts("dve", sl_["kd"].t[:], k_tok.t[:, tc], EGL.t[:, tile * 8 + gcol:tile * 8 + gcol + 1], None, ALU.mult, reads=[k_tok, EGL], writes=[sl_["kd"]])
                for cc in range(2):
                    glc = 256 + (cc * 64 + 63 if d == 0 else cc * 64)
                    S.cp("pool", sl_["gl"].t[:, cc:cc + 1], e3.t[:, glc:glc + 1], reads=[e3], writes=[sl_["gl"]])
                yield
                S.cp("act", z.t[:, 0:128], P1.t[:, 384:512], reads=[P1], writes=[z])
                yield
                for m in range(5):
                    S.mm(P2.t[:, 128:384], z.t[:, 0:128], z.t[:, 128:384], reads=[z], writes=[P2])
                    S.mm(P2.t[:, 0:128], z.t[:, 128:256], z.t[:, 0:128], reads=[z], writes=[P2])
                    yield
                    zn = ln["z"].next()
                    S.cp("act", zn.t[:, 0:256], P2.t[:, 0:256], reads=[P2], writes=[zn])
                    S.tt("dve", zn.t[:, 256:384], z.t[:, 256:384], P2.t[:, 256:384], ALU.add, reads=[z, P2], writes=[zn])
                    yield
                    z = zn
                S.mm(P2.t[:, 256:384], z.t[:, 0:128], z.t[:, 256:384], reads=[z], writes=[P2])
                yield
                S.tt("dve", sl_["Rb"].t[:], z.t[:, 256:384], P2.t[:, 256:384], ALU.add, reads=[z, P2], writes=[sl_["Rb"]])
                yield
                prep_done.add((i, d))

            def scan_gen(d):
                P3 = pS[d]
                for i in range(nt):
                    while (i, d) not in prep_done:
                        yield
                    tile = i if d == 0 else nt - 1 - i
                    gcol = d * 4 + h
                    tc = slice(tile * 128, (tile + 1) * 128)
                    sl_ = slots[d][i % DEPTH]
                    Rb, AT, kgT, qdT, kd, gl = sl_["Rb"], sl_["AT"], sl_["kgT"], sl_["qdT"], sl_["kd"], sl_["gl"]
                    for cc in ((0, 1) if d == 0 else (1, 0)):
                        r0 = cc * 64
                        rs = slice(r0, r0 + 64)
                        S.mm(P3.t[rs, 0:128], kgT.t[:, rs], Sb[d].t[:], reads=[kgT, Sb[d]], writes=[P3])
                        yield
                        rr = rr_r[d].next()
                        S.tt("dve", rr.t[rs, :], v_tok.t[rs, tc], P3.t[rs, 0:128], ALU.subtract, reads=[v_tok, P3], writes=[rr])
                        yield
                        S.mm(P3.t[rs, 128:256], Rb.t[rs, rs], rr.t[rs, :], reads=[Rb, rr], writes=[P3])
                        yield
                        vn = vn_r[d].next()
                        S.act(vn.t[rs, :], P3.t[rs, 128:256], AF.Copy, reads=[P3, BT], writes=[vn], scale=BT.t[rs, tile, gcol:gcol + 1])
                        yield
                        S.mm(P3.t[rs, 256:384], qdT.t[:, rs], Sb[d].t[:], start=True, stop=False, reads=[qdT, Sb[d]], writes=[P3])
                        S.mm(P3.t[rs, 256:384], AT.t[rs, rs], vn.t[rs, :], start=False, stop=True, reads=[AT, vn], writes=[P3])
                        S.mm(P3.t[:, 384:512], kd.t[rs, :], vn.t[rs, :], reads=[kd, vn], writes=[P3])
                        yield
                        S.stt(Sb[d].t[:], Sf[d].t[:], gl.t[:, cc:cc + 1], P3.t[:, 384:512], ALU.mult, ALU.add, reads=[Sf[d], gl, P3], writes=[Sb[d]])
                        S.stt(Sf[d].t[:], Sf[d].t[:], gl.t[:, cc:cc + 1], P3.t[:, 384:512], ALU.mult, ALU.add, reads=[Sf[d], gl, P3], writes=[Sf[d]])
                        yield
                    if tile not in oacc_written:
                        oacc_written.add(tile)
                        S.cp("act", oacc.t[:, tc], P3.t[:, 256:384], reads=[P3], writes=[oacc])
                    else:
                        S.tt("dve", oacc.t[:, tc], oacc.t[:, tc], P3.t[:, 256:384], ALU.add, reads=[oacc, P3], writes=[oacc])
                    scan_emitted[d] = i + 1
                    yield

            plist = [(i, d) for i in range(nt) for d in range(2)]
            pnext = 0
            lane_gen = [None] * NL
            scans = [scan_gen(0), scan_gen(1)]
            alive = [True, True]
            while pnext < len(plist) or any(g is not None for g in lane_gen) or any(alive):
                for l in range(NL):
                    if lane_gen[l] is None and pnext < len(plist):
                        i_, d_ = plist[pnext]
                        if i_ < scan_emitted[d_] + DEPTH:
                            lane_gen[l] = prep_gen(i_, d_, lanes[l])
                            pnext += 1
                    if lane_gen[l] is not None:
                        try:
                            next(lane_gen[l])
                        except StopIteration:
                            lane_gen[l] = None
                for d in range(2):
                    if alive[d]:
                        try:
                            next(scans[d])
                        except StopIteration:
                            alive[d] = False
            S.barrier()

            for t in range(nt):
                S.op("dve", "scalar_tensor_tensor", dict(out=junk.t[:], in0=oacc.t[:, t * 128:(t + 1) * 128], scalar=1.0, in1=oacc.t[:, t * 128:(t + 1) * 128],
                                                         op0=ALU.mult, op1=ALU.mult, accum_out=ssq.t[:, t:t + 1]), reads=[oacc], writes=[junk, ssq])
            S.act(rstd.t[:, 0:nt], ssq.t[:, 0:nt], AF.Ln, reads=[ssq], writes=[rstd], bias=128.0e-6)
            S.act(rstd.t[:, 0:nt], rstd.t[:, 0:nt], AF.Exp, reads=[rstd], writes=[rstd], scale=-0.5)
            for g0 in range(0, nt, 16):
                n = min(16, nt - g0)
                Z = z_r.next()
                S.dma("sp", Z.t[:, 0:n, :], k.zs[s0 + g0 * 128:s0 + (g0 + n) * 128, h * 128:(h + 1) * 128].rearrange("(n p) c -> p n c", p=128), writes=[Z], owner=Z)
                S.act(Z.t[:, 0:n, :], Z.t[:, 0:n, :], AF.Silu, reads=[Z], writes=[Z])
                S.tt("pool", Z.t[:, 0:n, :], Z.t[:, 0:n, :], gg.t[:, 0:128].unsqueeze(1).broadcast_to([128, n, 128]), ALU.mult, reads=[Z, gg], writes=[Z])
                for t in range(g0, g0 + n):
                    on = on_r.next()
                    S.stt(on.t[:], oacc.t[:, t * 128:(t + 1) * 128], rstd.t[:, t:t + 1], Z.t[:, t - g0, :], ALU.mult, ALU.mult, reads=[oacc, rstd, Z], writes=[on])
                    trb = k.psum[6 + (t // 4) % 2]
                    tv = bview(trb)
                    S.tr(tv[:, (t % 4) * 128:(t % 4 + 1) * 128], on.t[:], ident_b.t[:], reads=[on, ident_b], writes=[trb])
                    if t % 4 == 3:
                        ob = ob_r.next()
                        evac(ob.t[:], tv[:, 0:512], [trb], [ob])
                        S.dma("sp", k.mixT[h * 128:(h + 1) * 128, s0 + (t - 3) * 128:s0 + (t + 1) * 128], ob.t[:], reads=[ob], owner=ob)


def seq_starts(k):
    out, s0 = [], 0
    for sl in k.seqs:
        out.append((s0, sl))
        s0 += sl
    return out


def phase_C(k):
    nc, S = k.nc, k.S
    ar = Arena(nc)
    smax = k.smax
    ntmax = smax // 128
    ident_b = ar.alloc("ident_b", [128, 128], BF16)
    lq = ar.alloc("lq", [128, 256], F32)
    pr = ar.alloc("pr", [128, 128], F32)
    sm = ar.alloc("sm", [128, 8], F32)
    nlam = ar.alloc("nlam", [128, 1], F32)
    gt = ar.alloc("gt", [128, 128], F32)
    qT_r = ar.ring("qT", 2, [128, smax], BF16)
    kT_r = ar.ring("kT", 2, [128, smax], BF16)
    va_r = ar.ring("va", 2, [128, ntmax, 129], BF16)
    pT_r = ar.ring("pT", 3, [128, 1024], BF16)
    accs_r = ar.ring("accs", 2, [128, 3, 387], F32)
    fin_r = ar.ring("fin", 8, [128, 8], F32)
    t_r = ar.ring("tt", 4, [128, 128], F32)
    o_r = ar.ring("oo", 8, [128, 128], F32)
    junk = ar.alloc("junk", [128, 128], F32)
    on_r = ar.ring("on", 4, [128, 128], BF16)
    ob_r = ar.ring("obT", 2, [128, 512], BF16)
    trb = k.psum[7]
    acc_banks = k.psum[4:7]

    S.dma("sp", ident_b.t[:], k.c_ident_b, writes=[ident_b], owner=ident_b)
    S.dma("sp", lq.t[:], k.lam_qk[0:1, :].broadcast_to([128, 256]), writes=[lq], owner=lq)
    S.dma("sp", gt.t[:], k.diff_g[0:1, :].broadcast_to([128, 128]), writes=[gt], owner=gt)
    S.tt("dve", pr.t[:, 0:64], lq.t[:, 0:64], lq.t[:, 64:128], ALU.mult, reads=[lq], writes=[pr])
    S.tt("dve", pr.t[:, 64:128], lq.t[:, 128:192], lq.t[:, 192:256], ALU.mult, reads=[lq], writes=[pr])
    S.op("dve", "reduce_sum", dict(out=sm.t[:, 0:1], in_=pr.t[:, 0:64], axis=AX.X), reads=[pr], writes=[sm])
    S.op("dve", "reduce_sum", dict(out=sm.t[:, 1:2], in_=pr.t[:, 64:128], axis=AX.X), reads=[pr], writes=[sm])
    S.act(sm.t[:, 2:4], sm.t[:, 0:2], AF.Exp, reads=[sm], writes=[sm])
    S.tt("dve", sm.t[:, 4:5], sm.t[:, 3:4], sm.t[:, 2:3], ALU.subtract, reads=[sm], writes=[sm])
    S.ts("dve", nlam.t[:], sm.t[:, 4:5], -LAM_INIT, None, ALU.add, reads=[sm], writes=[nlam])
    S.ts("dve", gt.t[:], gt.t[:], (1.0 - LAM_INIT) * math.sqrt(128.0), None, ALU.mult, reads=[gt], writes=[gt])
    for va in va_r.items:
        S.memset("pool", va.t[:, :, 128:129], 1.0, writes=[va])

    jobs = [(s0, sl, h) for (s0, sl) in seq_starts(k) for h in range(4)]

    def load(job):
        s0, sl, h = job
        qT, kT, va = qT_r.next(), kT_r.next(), va_r.next()
        S.dma("sp", qT.t[:, 0:sl], k.dqT[h * 128:(h + 1) * 128, s0:s0 + sl], writes=[qT], owner=qT)
        S.dma("sp", kT.t[:, 0:sl], k.dkT[h * 128:(h + 1) * 128, s0:s0 + sl], writes=[kT], owner=kT)
        nt = sl // 128
        for c0 in range(0, nt, 8):
            c1 = min(nt, c0 + 8)
            S.dma("sp", va.t[:, c0:c1, 0:128], k.dvs[s0 + c0 * 128:s0 + c1 * 128, h * 128:(h + 1) * 128].rearrange("(n p) d -> p n d", p=128),
                  writes=[va], owner=va)
        return qT, kT, va

    pending = []

    def fin_gen(accs, h, t0):
        fins, oos = [], []
        for qi in range(4):
            a0, a1 = qi, 4 + qi
            A0 = accs.t[:, a0 // 3, (a0 % 3) * 129:(a0 % 3) * 129 + 129]
            A1 = accs.t[:, a1 // 3, (a1 % 3) * 129:(a1 % 3) * 129 + 129]
            fin = fin_r.next()
            S.op("dve", "reciprocal", dict(out=fin.t[:, 0:1], in_=A0[:, 128:129]), reads=[accs], writes=[fin])
            S.op("dve", "reciprocal", dict(out=fin.t[:, 1:2], in_=A1[:, 128:129]), reads=[accs], writes=[fin])
            S.ts("dve", fin.t[:, 2:3], fin.t[:, 1:2], nlam.t[:, 0:1], None, ALU.mult, reads=[fin, nlam], writes=[fin])
            tt = t_r.next()
            S.ts("dve", tt.t[:], A1[:, 0:128], fin.t[:, 2:3], None, ALU.mult, reads=[accs, fin], writes=[tt])
            oo = o_r.next()
            S.stt(oo.t[:], A0[:, 0:128], fin.t[:, 0:1], tt.t[:], ALU.mult, ALU.add, reads=[accs, fin, tt], writes=[oo])
            S.op("dve", "scalar_tensor_tensor", dict(out=junk.t[:], in0=oo.t[:], scalar=1.0, in1=oo.t[:], op0=ALU.mult, op1=ALU.mult, accum_out=fin.t[:, 3:4]),
                 reads=[oo], writes=[junk, fin])
            fins.append(fin)
            oos.append(oo)
        yield
        for qi in range(4):
            fin = fins[qi]
            S.act(fin.t[:, 5:6], fin.t[:, 3:4], AF.Ln, reads=[fin], writes=[fin], bias=128.0 * 1e-6)
            S.act(fin.t[:, 4:5], fin.t[:, 5:6], AF.Exp, reads=[fin], writes=[fin], scale=-0.5)
        yield
        obT = ob_r.next()
        for qi in range(4):
            on = on_r.next()
            S.stt(on.t[:], oos[qi].t[:], fins[qi].t[:, 4:5], gt.t[:], ALU.mult, ALU.mult, reads=[oos[qi], fins[qi], gt], writes=[on])
            trv = trb.t[:, qi * 64:(qi + 1) * 64].bitcast(BF16)
            S.tr(trv, on.t[:], ident_b.t[:], reads=[on, ident_b], writes=[trb])
        yield
        S.cp("dve", obT.t[:], trb.t[:, 0:256].bitcast(BF16), reads=[trb], writes=[obT])
        S.dma("sp", k.mixT[512 + h * 128:512 + (h + 1) * 128, t0:t0 + 512], obT.t[:], reads=[obT], owner=obT)

    pair_i = [0]
    nxt = load(jobs[0])
    for ji, (s0, sl, h) in enumerate(jobs):
        qT, kT, va = nxt
        if ji + 1 < len(jobs):
            nxt = load(jobs[ji + 1])
        nt = sl // 128
        for qb in range(sl // 512):
            scs = {}

            def score(kt):
                p = pair_i[0] % 2
                pair_i[0] += 1
                for comp in range(2):
                    sc = k.psum[2 * p + comp]
                    S.mm(sc.t[:, :], kT.t[comp * 64:(comp + 1) * 64, kt * 128:(kt + 1) * 128], qT.t[comp * 64:(comp + 1) * 64, qb * 512:(qb + 1) * 512],
                         reads=[kT, qT], writes=[sc])
                scs[kt] = p

            started = set()
            score(0)
            for kt in range(nt):
                p = scs.pop(kt)
                pT = pT_r.next()
                S.act(pT.t[:], k.pspair[p][:, 0:1024], AF.Exp, reads=[k.psum[2 * p], k.psum[2 * p + 1]], writes=[pT])
                if kt + 1 < nt:
                    score(kt + 1)
                if pending and kt % 2 == 1:
                    try:
                        next(pending[0])
                    except StopIteration:
                        pending.pop(0)
                for comp in range(2):
                    for qi in range(4):
                        ai = comp * 4 + qi
                        bank = acc_banks[ai // 3]
                        c0 = (ai % 3) * 129
                        st = (kt == 0) and (ai // 3 not in started)
                        started.add(ai // 3)
                        S.mm(bank.t[:, c0:c0 + 129], pT.t[:, comp * 512 + qi * 128:comp * 512 + (qi + 1) * 128], va.t[:, kt, :], start=st, stop=(kt == nt - 1),
                             reads=[pT, va], writes=[bank])
            accs = accs_r.next()
            S.cp("act", accs.t[:, 0, :], acc_banks[0].t[:, 0:387], reads=[acc_banks[0]], writes=[accs])
            S.cp("dve", accs.t[:, 1, :], acc_banks[1].t[:, 0:387], reads=[acc_banks[1]], writes=[accs])
            S.cp("act", accs.t[:, 2, 0:258], acc_banks[2].t[:, 0:258], reads=[acc_banks[2]], writes=[accs])
            while pending:
                g = pending.pop(0)
                for _ in g:
                    pass
            pending.append(fin_gen(accs, h, s0 + qb * 512))
    while pending:
        g = pending.pop(0)
        for _ in g:
            pass


TD = 128


def phase_D(k):
    nc, S, T = k.nc, k.S, k.T
    ar = Arena(nc)
    w_out = ar.alloc("w_out", [128, 8, 1024], BF16)
    w_gu = ar.alloc("w_gu", [128, 8, 2 * D_FF], BF16)
    w_dn = ar.alloc("w_dn", [128, 22, 1024], BF16)
    lnp = ar.alloc("lnp", [128, 4, 1024], F32)
    ident_f = ar.alloc("ident_f", [128, 128], F32)
    mark = ar.off
    wst = ar.ring("wst", 2, [128, D_FF], F32)
    S.dma("sp", ident_f.t[:], k.c_ident_f, writes=[ident_f], owner=ident_f)
    for i, p in enumerate((k.ln1_g, k.ln1_b, k.ln2_g, k.ln2_b)):
        S.dma("sp", lnp.t[:, i, :], p[0:1, :].broadcast_to([128, 1024]), writes=[lnp], owner=lnp)
    ci = [0]

    def cast(out_ap, st, n, wt):
        S.cp(["dve", "pool", "act"][ci[0] % 3], out_ap, st.t[:, 0:n], reads=[st], writes=[wt])
        ci[0] += 1

    wo_v = k.w_out.rearrange("(kk p) c -> kk p c", p=128)
    for kk in range(8):
        st = wst.next()
        S.dma("sp", st.t[:, 0:1024], wo_v[kk], writes=[st], owner=st)
        cast(w_out.t[:, kk, :], st, 1024, w_out)
    wg_v = k.w_gu.rearrange("(kk p) c -> kk p c", p=128)
    for kk in range(8):
        for half in range(2):
            st = wst.next()
            S.dma("sp", st.t[:, :], wg_v[kk][:, half * D_FF:(half + 1) * D_FF], writes=[st], owner=st)
            cast(w_gu.t[:, kk, half * D_FF:(half + 1) * D_FF], st, D_FF, w_gu)
    wd_v = k.w_down.rearrange("(j p) c -> j p c", p=128)
    for j in range(22):
        st = wst.next()
        S.dma("sp", st.t[:, 0:1024], wd_v[j], writes=[st], owner=st)
        cast(w_dn.t[:, j, :], st, 1024, w_dn)
    S.barrier()
    ar.off = mark
    mx_r = ar.ring("mixT", 3, [128, 8, TD], BF16)
    xx_r = ar.ring("x", 2, [128, 1024], F32)
    x1_r = ar.ring("x1", 2, [128, 1024], F32)
    x1T_r = ar.ring("x1T", 2, [128, 8, TD], BF16)
    aT_r = ar.ring("aT", 2, [128, 22, TD], BF16)
    st_r = ar.ring("bst", 4, [128, 2, 6], F32)
    mv_r = ar.ring("mv", 4, [128, 4], F32)
    sl_r = ar.ring("silu", 3, [128, TD], F32)
    banks = Ring(k.psum[0:8])
    mix_v = k.mixT.rearrange("(kk p) t -> p kk t", p=128)
    ntile = T // TD
    ev = [0]

    def layer_norm(src, dst, gi):
        bst, mv = st_r.next(), mv_r.next()
        for hf in range(2):
            S.op("dve", "bn_stats", dict(out=bst.t[:, hf, :], in_=src.t[:, hf * 512:(hf + 1) * 512]), reads=[src], writes=[bst])
        S.op("dve", "bn_aggr", dict(out=mv.t[:, 0:2], in_=bst.t[:, :, :]), reads=[bst], writes=[mv])
        S.act(mv.t[:, 3:4], mv.t[:, 1:2], AF.Ln, reads=[mv], writes=[mv], bias=1e-5)
        S.act(mv.t[:, 2:3], mv.t[:, 3:4], AF.Exp, reads=[mv], writes=[mv], scale=-0.5)
        S.ts("dve", dst.t[:], src.t[:], mv.t[:, 0:1], mv.t[:, 2:3], ALU.subtract, ALU.mult, reads=[src, mv], writes=[dst])
        S.tt("pool", dst.t[:], dst.t[:], lnp.t[:, gi, :], ALU.mult, reads=[dst, lnp], writes=[dst])
        S.tt("pool", dst.t[:], dst.t[:], lnp.t[:, gi + 1, :], ALU.add, reads=[dst, lnp], writes=[dst])

    st8 = {}

    def loads(i):
        t0 = i * TD
        mx, xx = mx_r.next(), xx_r.next()
        S.dma("sp", mx.t[:], mix_v[:, :, t0:t0 + TD], writes=[mx], owner=mx)
        S.dma("sp", xx.t[:], k.x[t0:t0 + TD, :], writes=[xx], owner=xx)
        st8[i] = dict(mx=mx, xx=xx)

    def out_proj(i):
        d = st8[i]
        mx, xx = d["mx"], d["xx"]
        x1 = x1_r.next()
        d["x1"] = x1
        for hf in range(2):
            bank = banks.next()
            for kk in range(8):
                S.mm(bank.t[:, :], mx.t[:, kk, :], w_out.t[:, kk, hf * 512:(hf + 1) * 512], start=(kk == 0), stop=(kk == 7), reads=[mx, w_out], writes=[bank])
            S.stt(xx.t[:, hf * 512:(hf + 1) * 512], xx.t[:, hf * 512:(hf + 1) * 512], ALPHA, bank.t[:, :], ALU.mult, ALU.add, reads=[xx, bank], writes=[xx])
        layer_norm(xx, x1, 0)

    def transposes(i):
        d = st8[i]
        x1 = d["x1"]
        x1T = x1T_r.next()
        d["x1T"] = x1T
        for k4 in range(2):
            bank = banks.next()
            for q in range(4):
                kk = k4 * 4 + q
                S.tr(bank.t[:, q * 128:(q + 1) * 128], x1.t[:, kk * 128:(kk + 1) * 128], ident_f.t[:], reads=[x1, ident_f], writes=[bank])
            S.cp("act" if ev[0] % 2 == 0 else "dve", x1T.t[:, k4 * 4:(k4 + 1) * 4, :], bank.t[:, 0:512], reads=[bank], writes=[x1T])
            ev[0] += 1

    def gate_up(i):
        d = st8[i]
        x1T = d["x1T"]
        aT = aT_r.next()
        d["aT"] = aT
        for j in range(22):
            gb = banks.next()
            for kk in range(8):
                S.mm(gb.t[:, 0:TD], w_gu.t[:, kk, j * 128:(j + 1) * 128], x1T.t[:, kk, :], start=(kk == 0), stop=(kk == 7), reads=[w_gu, x1T], writes=[gb])
            for kk in range(8):
                S.mm(gb.t[:, 256:256 + TD], w_gu.t[:, kk, D_FF + j * 128:D_FF + (j + 1) * 128], x1T.t[:, kk, :], start=False, stop=(kk == 7), reads=[w_gu, x1T], writes=[gb])
            sl = sl_r.next()
            S.act(sl.t[:], gb.t[:, 0:TD], AF.Silu, reads=[gb], writes=[sl])
            S.tt("dve", aT.t[:, j, :], sl.t[:], gb.t[:, 256:256 + TD], ALU.mult, reads=[sl, gb], writes=[aT])

    def down(i):
        d = st8.pop(i)
        x1, aT = d["x1"], d["aT"]
        t0 = i * TD
        for hf in range(2):
            bank = banks.next()
            for j in range(22):
                S.mm(bank.t[:, :], aT.t[:, j, :], w_dn.t[:, j, hf * 512:(hf + 1) * 512], start=(j == 0), stop=(j == 21), reads=[aT, w_dn], writes=[bank])
            S.stt(x1.t[:, hf * 512:(hf + 1) * 512], x1.t[:, hf * 512:(hf + 1) * 512], ALPHA, bank.t[:, :], ALU.mult, ALU.add, reads=[x1, bank], writes=[x1])
        layer_norm(x1, x1, 2)
        S.dma("sp", k.y[t0:t0 + TD, :], x1.t[:], reads=[x1], owner=x1)

    loads(0)
    if ntile > 1:
        loads(1)
    out_proj(0)
    transposes(0)
    for i in range(ntile):
        gate_up(i)
        if i + 2 < ntile:
            loads(i + 2)
        if i + 1 < ntile:
            out_proj(i + 1)
        down(i)
        if i + 1 < ntile:
            transposes(i + 1)


_CACHE = {}


def _get_program(seqs, debug=False, phases="ABCD"):
    key = (tuple(seqs), debug, phases)
    if key not in _CACHE:
        _CACHE[key] = build(list(seqs), debug, phases)
    return _CACHE[key]


def make_in_maps(seqs, n_cores, x_prompt, x_sample, w):
    smax = max(seqs)
    c = _consts(smax)
    per = len(seqs) - 1
    shared = {
        "w_in": np.ascontiguousarray(w["w_in"][0]), "conv_w": np.ascontiguousarray(w["conv_w"][0]),
        "a_log": np.ascontiguousarray(w["a_log"][0]).reshape(1, 8), "dt_bias": np.ascontiguousarray(w["dt_bias"][0]).reshape(1, 8),
        "gdn_norm_g": np.ascontiguousarray(w["gdn_norm_g"]).reshape(1, 128), "lam_qk": np.ascontiguousarray(w["lam_qk"][0]).reshape(1, 256),
        "diff_norm_g": np.ascontiguousarray(w["diff_norm_g"]).reshape(1, 128), "w_out": np.ascontiguousarray(w["w_out"][0]),
        "ln1_g": np.ascontiguousarray(w["ln1_g"]).reshape(1, 1024), "ln1_b": np.ascontiguousarray(w["ln1_b"]).reshape(1, 1024),
        "w_gate_up": np.ascontiguousarray(w["w_gate_up"][0]), "w_down": np.ascontiguousarray(w["w_down"][0]),
        "ln2_g": np.ascontiguousarray(w["ln2_g"]).reshape(1, 1024), "ln2_b": np.ascontiguousarray(w["ln2_b"]).reshape(1, 1024),
        "c_ident_f": c["ident_f"], "c_ident_b": c["ident_b"], "c_rope": c["rope"], "c_perm": c["perm"],
        "c_gm_f": c["gm_f"], "c_gm_mb": c["gm_mb"], "c_gm_neg": c["gm_neg"],
    }
    in_maps = []
    for i in range(n_cores):
        parts = [x_prompt[i]] + [x_sample[per * i + j] for j in range(per)]
        xc = np.ascontiguousarray(np.concatenate(parts, 0))
        m = dict(shared)
        m["x"] = xc
        in_maps.append(m)
    return in_maps


def kernel(x_prompt, x_sample, w_in, conv_w, a_log, dt_bias, gdn_norm_g, lam_qk, diff_norm_g,
           w_out, ln1_g, ln1_b, w_gate_up, w_down, ln2_g, ln2_b):
    n = 8
    x_prompt = np.asarray(x_prompt, np.float32)
    x_sample = np.asarray(x_sample, np.float32)
    w = dict(w_in=w_in, conv_w=conv_w, a_log=a_log, dt_bias=dt_bias, gdn_norm_g=gdn_norm_g, lam_qk=lam_qk,
             diff_norm_g=diff_norm_g, w_out=w_out, ln1_g=ln1_g, ln1_b=ln1_b, w_gate_up=w_gate_up, w_down=w_down,
             ln2_g=ln2_g, ln2_b=ln2_b)
    w = {kk: np.asarray(v, np.float32) for kk, v in w.items()}
    Sp = x_prompt.shape[1]
    Ss = x_sample.shape[1]
    per = x_sample.shape[0] // n
    seqs = [Sp] + [Ss] * per
    nc, kk = _get_program(seqs)
    in_maps = make_in_maps(seqs, n, x_prompt, x_sample, w)
    res = run_bass_kernel_spmd(nc, in_maps, core_ids=list(range(n)))
    yp = np.empty_like(x_prompt)
    ys = np.empty_like(x_sample)
    for i in range(n):
        y = res.results[i]["y"]
        yp[i] = y[:Sp]
        for j in range(per):
            ys[per * i + j] = y[Sp + j * Ss: Sp + (j + 1) * Ss]
    return yp, ys
```

```python
import math
import numpy as np
import ml_dtypes
import concourse.bass as bass
import concourse.mybir as mybir
from concourse.bass_utils import run_bass_kernel_spmd

F32 = mybir.dt.float32
BF16 = mybir.dt.bfloat16
AF = mybir.ActivationFunctionType
ALU = mybir.AluOpType
AX = mybir.AxisListType

D_MODEL = 1024
IN_COLS = 3600
D_FF = 2816
C_Z, C_G, C_DQ, C_DK, C_DV = 1536, 2048, 2176, 2688, 3200
W_COLS = 3712
ROPE_THETA = 500000.0
ALPHA = 2.0 ** 0.25
LAM_INIT = 0.8 - 0.6 * math.exp(-0.3 * 0)
NEG = -30000.0

ENGS = ["pe", "act", "dve", "pool", "sp"]
PE_ = "pool"
KCUT = 9
ROPEQ = "sp"
KR = 9


class Buf:
    __slots__ = ("name", "w", "r", "sem", "excl")

    def __init__(self, name="", excl=False):
        self.name = name
        self.excl = excl
        self.w = None
        self.r = []
        self.sem = None


class Sched:
    def __init__(self, nc):
        self.nc = nc
        self.ops = {e: [] for e in ENGS}
        self.nsem = 0
        self.semcount = {}
        self.barrier_tokens = {}

    def _newsem(self):
        k = self.nsem
        self.nsem += 1
        self.semcount[k] = 0
        return k

    def op(self, eng, meth, kw, reads=(), writes=(), dma_owner=None):
        fn = (meth, kw)
        reads = [r.b if isinstance(r, Tl) else r for r in reads]
        writes = [r.b if isinstance(r, Tl) else r for r in writes]
        if isinstance(dma_owner, Tl):
            dma_owner = dma_owner.b
        xr = [b for b in reads if b.excl and b not in writes]
        if xr:
            reads = [b for b in reads if not b.excl]
            writes = list(writes) + xr
        idx = len(self.ops[eng])
        waits = set()
        bt = self.barrier_tokens.pop(eng, None)
        if bt:
            waits |= bt
        for b in reads:
            t = b.w
            if t is not None:
                if t[0] == "c" and t[1] == eng and eng == "pe":
                    continue
                waits.add(t)
        for b in writes:
            t = b.w
            if t is not None and not (t[0] == "c" and t[1] == eng and eng == "pe"):
                waits.add(t)
            for t in b.r:
                if not (t[0] == "c" and t[1] == eng and eng == "pe"):
                    waits.add(t)
        if dma_owner is not None:
            if dma_owner.sem is None:
                dma_owner.sem = self._newsem()
            k = dma_owner.sem
            self.semcount[k] += 16
            tok = ("d", k, self.semcount[k])
        else:
            tok = ("c", eng, idx)
        for b in reads:
            b.r.append(tok)
        for b in writes:
            b.w = tok
            b.r = []
        self.ops[eng].append([fn, waits, tok])
        return tok

    def mm(self, out, lhsT, rhs, start=True, stop=True, reads=(), writes=()):
        return self.op("pe", "matmul", dict(out=out, lhsT=lhsT, rhs=rhs, start=start, stop=stop, skip_group_check=True), reads, writes)

    def tr(self, out, in_, ident, reads=(), writes=()):
        return self.op("pe", "transpose", dict(out=out, in_=in_, identity=ident), reads, writes)

    def dma(self, q, out, in_, reads=(), writes=(), owner=None):
        return self.op(q, "dma_start", dict(out=out, in_=in_), reads, writes, dma_owner=owner)

    def act(self, out, in_, func, reads=(), writes=(), **kw):
        d = dict(out=out, in_=in_, func=func)
        d.update(kw)
        return self.op("act", "activation", d, reads, writes)

    def cp(self, eng, out, in_, reads=(), writes=()):
        if eng == "act":
            return self.op("act", "copy", dict(out=out, in_=in_), reads, writes)
        return self.op(eng, "tensor_copy", dict(out=out, in_=in_), reads, writes)

    def tt(self, eng, out, in0, in1, op, reads=(), writes=()):
        return self.op(eng, "tensor_tensor", dict(out=out, in0=in0, in1=in1, op=op), reads, writes)

    def ts(self, eng, out, in0, s1, s2, op0, op1=None, reads=(), writes=(), **kw):
        d = dict(out=out, in0=in0, scalar1=s1, scalar2=s2, op0=op0)
        if op1 is not None:
            d["op1"] = op1
        d.update(kw)
        return self.op(eng, "tensor_scalar", d, reads, writes)

    def stt(self, out, in0, scalar, in1, op0, op1, reads=(), writes=(), eng="dve"):
        return self.op(eng, "scalar_tensor_tensor", dict(out=out, in0=in0, scalar=scalar, in1=in1, op0=op0, op1=op1), reads, writes)

    def memset(self, eng, ap, val, writes=()):
        return self.op(eng, "memset", dict(ap=ap, constant=val), (), writes)

    def barrier(self):
        toks = set()
        for e in ENGS:
            n = len(self.ops[e])
            for i in range(n - 1, -1, -1):
                if self.ops[e][i][2][0] == "c":
                    toks.add(("c", e, i))
                    break
        for k, v in self.semcount.items():
            if v > 0:
                toks.add(("d", k, v))
        for e in ENGS:
            cur = self.barrier_tokens.get(e, set())
            self.barrier_tokens[e] = cur | {t for t in toks if not (t[0] == "c" and t[1] == e)}

    def emit(self):
        nc = self.nc
        self.barrier()
        final_waits = self.barrier_tokens.pop("sp", set())
        self.barrier_tokens = {}
        targets = {e: set() for e in ENGS}
        allw = [final_waits]
        for e in ENGS:
            for fn, waits, tok in self.ops[e]:
                allw.append(waits)
        for waits in allw:
            for t in waits:
                if t[0] == "c":
                    targets[t[1]].add(t[2])
        rank = {}
        for e in ENGS:
            rank[e] = {i: r + 1 for r, i in enumerate(sorted(targets[e]))}
        esem = {e: nc.alloc_semaphore(name=f"es_{e}") for e in ENGS}
        dsem = {k: nc.alloc_semaphore(name=f"ds_{k}") for k in range(self.nsem)}
        stats = {e: [len(self.ops[e]), 0] for e in ENGS}

        def run(e, engobj, extra_waits=None):
            seen = {}

            def do_waits(waits):
                best = {}
                for t in waits:
                    if t[0] == "c":
                        key = ("c", t[1])
                        val = rank[t[1]][t[2]]
                    else:
                        key = ("d", t[1])
                        val = t[2]
                    if val > best.get(key, 0):
                        best[key] = val
                for key, val in best.items():
                    if seen.get(key, 0) >= val:
                        continue
                    seen[key] = val
                    sem = esem[key[1]] if key[0] == "c" else dsem[key[1]]
                    engobj.wait_ge(sem, val)
                    stats[e][1] += 1

            for i, (fn, waits, tok) in enumerate(self.ops[e]):
                do_waits(waits)
                ins = getattr(engobj, fn[0])(**fn[1])
                if tok[0] == "d":
                    ins.then_inc(dsem[tok[1]], 16)
                elif i in rank[e]:
                    ins.then_inc(esem[e], 1)
            if extra_waits:
                do_waits(extra_waits)

        with nc.Block() as block:
            @block.tensor
            def _(eng):
                run("pe", eng)

            @block.scalar
            def _(eng):
                run("act", eng)

            @block.vector
            def _(eng):
                run("dve", eng)

            @block.gpsimd
            def _(eng):
                run("pool", eng)

            @block.sync
            def _(eng):
                run("sp", eng, final_waits)
        return stats


class Tl:
    __slots__ = ("t", "b")

    def __init__(self, t, name="", excl=False):
        self.t = t
        self.b = Buf(name, excl)


_DS = {F32: 4, BF16: 2}


class BankView:
    def __init__(self, t, off):
        self.t_, self.off = t, off

    def __getitem__(self, key):
        r, c = key
        a = (c.start or 0) + self.off
        b = (c.stop if c.stop is not None else 512) + self.off
        return self.t_[r, a:b]


class Arena:
    def __init__(self, nc, base=16512, limit=229344):
        self.nc = nc
        self.off = base
        self.limit = limit
        self.n = 0

    def alloc(self, name, shape, dtype):
        sz = int(np.prod(shape[1:])) * _DS[dtype]
        sz = (sz + 63) // 64 * 64
        assert self.off + sz <= self.limit, f"SBUF overflow at {name}: {self.off}+{sz}"
        self.n += 1
        t = self.nc.alloc_sbuf_tensor_at(f"{name}{self.n}", list(shape), dtype, offset=self.off)
        self.off += sz
        return Tl(t, name)

    def ring(self, name, n, shape, dtype):
        return Ring([self.alloc(f"{name}{i}_", shape, dtype) for i in range(n)])


class Ring:
    def __init__(self, items):
        self.items = items
        self.i = 0

    def next(self):
        it = self.items[self.i % len(self.items)]
        self.i += 1
        return it


def _consts(smax):
    c = {}
    c["ident_f"] = np.eye(128, dtype=np.float32)
    c["ident_b"] = np.eye(128, dtype=np.float32).astype(ml_dtypes.bfloat16)
    inv = (1.0 / (np.float32(ROPE_THETA) ** (np.arange(0, 16, 2, dtype=np.float32) / np.float32(16)))).astype(np.float32)
    ang = (np.arange(smax, dtype=np.float32)[:, None] * inv[None, :]).astype(np.float32)
    cos, sin = np.cos(ang).astype(np.float32), np.sin(ang).astype(np.float32)
    COS = np.ones((128, smax), np.float32)
    SIN = np.zeros((128, smax), np.float32)
    perm = np.zeros((128, 128), np.float32)
    for comp in range(2):
        b = comp * 64
        for p in range(8):
            COS[b + p] = cos[:, p]
            COS[b + 8 + p] = cos[:, p]
            SIN[b + p] = -sin[:, p]
            SIN[b + 8 + p] = sin[:, p]
            perm[b + 8 + p, b + p] = 1.0
            perm[b + p, b + 8 + p] = 1.0
    c["rope"] = np.stack([COS * 0.125, SIN * 0.125, COS, SIN], 0).astype(np.float32)
    c["perm"] = perm.astype(ml_dtypes.bfloat16)
    t = np.arange(128)
    same = (t[:, None] // 64) == (t[None, :] // 64)
    le = t[:, None] <= t[None, :]
    ge = t[:, None] >= t[None, :]
    lt = t[:, None] < t[None, :]
    gt = t[:, None] > t[None, :]
    gm = {}
    for d, (ma, mb, strict_ji, incl_ji, mac) in enumerate([
        (le & same, gt & same, lt & same, le & same, gt & same),
        (ge & same, lt & same, gt & same, ge & same, lt & same),
    ]):
        gm[f"ma2_{d}"] = np.concatenate([ma, ma], 1).astype(np.float32)
        gm[f"mb_{d}"] = mb.astype(np.float32)
        gm[f"neg2_{d}"] = np.concatenate([np.where(strict_ji, 0.0, NEG), np.where(incl_ji, 0.0, NEG)], 1).astype(ml_dtypes.bfloat16)
        gm[f"mac_{d}"] = mac.astype(np.float32)
    c["gm_f"] = np.stack([gm["ma2_0"], gm["ma2_1"]], 0)
    c["gm_mb"] = np.stack([gm["mb_0"], gm["mb_1"], gm["mac_0"], gm["mac_1"]], 0)
    c["gm_neg"] = np.stack([gm["neg2_0"], gm["neg2_1"]], 0)
    return c


class K:
    pass


def build(seqs, debug=False, phases="ABCD"):
    T = sum(seqs)
    smax = max(seqs)
    nc = bass.Bass("TRN2", target_bir_lowering=False)
    k = K()
    k.nc, k.T, k.seqs, k.smax = nc, T, seqs, smax
    k.S = Sched(nc)

    def din(name, shape, dt=F32):
        return nc.dram_tensor(name, list(shape), dt, kind="ExternalInput").ap()

    k.x = din("x", [T, D_MODEL])
    k.w_in = din("w_in", [D_MODEL, IN_COLS])
    k.conv_w = din("conv_w", [5, 1536])
    k.a_log = din("a_log", [1, 8])
    k.dt_bias = din("dt_bias", [1, 8])
    k.gdn_g = din("gdn_norm_g", [1, 128])
    k.lam_qk = din("lam_qk", [1, 256])
    k.diff_g = din("diff_norm_g", [1, 128])
    k.w_out = din("w_out", [1024, 1024])
    k.ln1_g = din("ln1_g", [1, 1024])
    k.ln1_b = din("ln1_b", [1, 1024])
    k.w_gu = din("w_gate_up", [1024, 2 * D_FF])
    k.w_down = din("w_down", [D_FF, 1024])
    k.ln2_g = din("ln2_g", [1, 1024])
    k.ln2_b = din("ln2_b", [1, 1024])
    k.c_ident_f = din("c_ident_f", [128, 128])
    k.c_ident_b = din("c_ident_b", [128, 128], BF16)
    k.c_rope = din("c_rope", [4, 128, smax])
    k.c_perm = din("c_perm", [128, 128], BF16)
    k.c_gm_f = din("c_gm_f", [2, 128, 256])
    k.c_gm_mb = din("c_gm_mb", [4, 128, 128])
    k.c_gm_neg = din("c_gm_neg", [2, 128, 256], BF16)

    skind = "ExternalOutput" if debug else "Internal"

    def dscr(name, shape, dt):
        return nc.dram_tensor(name, list(shape), dt, kind=skind).ap()

    k.qkvT = dscr("s_qkvT", [1536, T], BF16)
    k.zs = dscr("s_z", [T, 512], F32)
    k.gs = dscr("s_g", [T, 16], F32)
    k.dqT = dscr("s_dqT", [512, T], BF16)
    k.dkT = dscr("s_dkT", [512, T], BF16)
    k.dvs = dscr("s_dv", [T, 512], BF16)
    k.mixT = dscr("s_mixT", [1024, T], BF16)
    k.y = nc.dram_tensor("y", [T, D_MODEL], F32, kind="ExternalOutput").ap()
    k.pspair = [nc.alloc_psum_tensor(f"pp{i}", [128, 1024], F32) for i in range(4)]
    k.psum = [Tl(BankView(k.pspair[i // 2], (i % 2) * 512), f"ps{i}", excl=True) for i in range(8)]

    if "A" in phases:
        phase_A(k)
        k.S.barrier()
    if "B" in phases:
        phase_B(k)
        k.S.barrier()
    if "C" in phases:
        phase_C(k)
        k.S.barrier()
    if "D" in phases:
        phase_D(k)
    stats = k.S.emit()
    k.stats = stats
    return nc, k


def phase_A(k):
    nc, S, T = k.nc, k.S, k.T
    ar = Arena(nc)
    w_bf = ar.alloc("w_in_bf", [128, 8, W_COLS], BF16)
    wst = ar.ring("wst", 2, [128, IN_COLS], F32)
    ident_f = ar.alloc("ident_f", [128, 128], F32)
    perm = ar.alloc("perm", [128, 128], BF16)
    xs_r = ar.ring("xs", 2, [128, 4, 1024], F32)
    xT_r = ar.ring("xT", 2, [128, 8, 512], BF16)
    sb_r = ar.ring("stb", 6, [128, 512], BF16)
    sf_r = ar.ring("stf", 4, [128, 512], F32)
    rope_r = ar.ring("rope", 2, [128, 4, 512], F32)
    zst_r = ar.ring("zst", 2, [128, 4, 512], F32)
    dvst_r = ar.ring("dvst", 2, [128, 4, 512], BF16)
    gst_r = ar.ring("gst", 2, [128, 4, 16], F32)
    tr_banks = Ring(k.psum[0:2])
    mm_banks = Ring(k.psum[2:8])

    S.dma("sp", ident_f.t[:], k.c_ident_f, writes=[ident_f], owner=ident_f)
    S.dma("sp", perm.t[:], k.c_perm, writes=[perm], owner=perm)
    w_view = k.w_in.rearrange("(kk p) c -> kk p c", p=128)
    for kk in range(8):
        st = wst.next()
        S.dma("sp", st.t[:], w_view[kk], writes=[st], owner=st)
        S.cp(["dve", PE_, "act"][kk % 3], w_bf.t[:, kk, 0:2064], st.t[:, 0:2064], reads=[st], writes=[w_bf])
        S.cp(["dve", PE_, "act"][(kk + 1) % 3], w_bf.t[:, kk, C_DQ:C_DQ + 1536], st.t[:, 2064:3600], reads=[st], writes=[w_bf])

    tiles = []
    s0 = 0
    for sl in k.seqs:
        for p0 in range(0, sl, 512):
            tiles.append((s0 + p0, p0))
        s0 += sl

    def load_x(i):
        t0, _ = tiles[i]
        xs = xs_r.next()
        S.dma("sp", xs.t[:], k.x[t0:t0 + 512, :].rearrange("(a p) d -> p a d", p=128), writes=[xs], owner=xs)
        return xs

    ev = [0]

    def evac(out_ap, in_ap, reads, writes):
        S.cp("act" if ev[0] % 2 == 0 else "dve", out_ap, in_ap, reads, writes)
        ev[0] += 1

    nxt = load_x(0)
    for i, (t0, pos0) in enumerate(tiles):
        xs = nxt
        if i + 1 < len(tiles):
            nxt = load_x(i + 1)
        rope = rope_r.next()
        S.dma(ROPEQ, rope.t[:], k.c_rope[:, :, pos0:pos0 + 512].rearrange("f p s -> p f s"), writes=[rope], owner=rope)
        xT = xT_r.next()
        for kk in range(8):
            bank = tr_banks.next()
            for a in range(4):
                S.tr(bank.t[:, a * 128:(a + 1) * 128], xs.t[:, a, kk * 128:(kk + 1) * 128], ident_f.t[:], reads=[xs, ident_f], writes=[bank])
            evac(xT.t[:, kk, :], bank.t[:, :], [bank], [xT])

        def proj_fm(col0):
            bank = mm_banks.next()
            for kk in range(8):
                S.mm(bank.t[:, :], w_bf.t[:, kk, col0:col0 + 128], xT.t[:, kk, :], start=(kk == 0), stop=(kk == 7), reads=[w_bf, xT], writes=[bank])
            return bank

        for c in range(12):
            bank = proj_fm(c * 128)
            st = sb_r.next()
            evac(st.t[:], bank.t[:, :], [bank], [st])
            S.dma("sp", k.qkvT[c * 128:(c + 1) * 128, t0:t0 + 512], st.t[:], reads=[st], owner=st)
        for qk in range(2 if KCUT >= 2 else 0):
            dst = k.dqT if qk == 0 else k.dkT
            cbase = C_DQ if qk == 0 else C_DK
            for h in range(4):
                bank = proj_fm(cbase + h * 128)
                rawb = sb_r.next()
                S.cp("act", rawb.t[:], bank.t[:, :], reads=[bank], writes=[rawb])
                t1 = sf_r.next()
                if KR >= 2:
                    S.tt("dve", t1.t[:], rope.t[:, 2 * qk, :], bank.t[:, :], ALU.mult, reads=[bank, rope], writes=[t1])
                bank2 = mm_banks.next()
                t2 = sf_r.next()
                if KR >= 3:
                    S.mm(bank2.t[:, :], perm.t[:], rawb.t[:], reads=[perm, rawb], writes=[bank2])
                    S.tt("dve", t2.t[:], bank2.t[:, :], rope.t[:, 2 * qk + 1, :], ALU.mult, reads=[bank2, rope], writes=[t2])
                ob = sb_r.next()
                if KR >= 4:
                    S.tt(PE_, ob.t[:], t1.t[:], t2.t[:], ALU.add, reads=[t1, t2], writes=[ob])
                else:
                    ob = rawb
                S.dma("sp", dst[h * 128:(h + 1) * 128, t0:t0 + 512], ob.t[:], reads=[ob], owner=ob)
        zst, dvst, gst = zst_r.next(), dvst_r.next(), gst_r.next()
        for a in range(4 if KCUT >= 3 else 0):
            for (col0, ncol, dstt) in ((C_Z, 512, zst), (C_DV, 512, dvst), (C_G, 16, gst))[:KCUT - 2]:
                bank = mm_banks.next()
                for kk in range(8):
                    S.mm(bank.t[:, 0:ncol], xT.t[:, kk, a * 128:(a + 1) * 128], w_bf.t[:, kk, col0:col0 + ncol], start=(kk == 0), stop=(kk == 7), reads=[w_bf, xT], writes=[bank])
                evac(dstt.t[:, a, :], bank.t[:, 0:ncol], [bank], [dstt])
        if KCUT >= 3:
            S.dma("sp", k.zs[t0:t0 + 512, :].rearrange("(a p) c -> p a c", p=128), zst.t[:], reads=[zst], owner=zst)
        if KCUT >= 4:
            S.dma("sp", k.dvs[t0:t0 + 512, :].rearrange("(a p) c -> p a c", p=128), dvst.t[:], reads=[dvst], owner=dvst)
        if KCUT >= 5:
            S.dma("sp", k.gs[t0:t0 + 512, :].rearrange("(a p) c -> p a c", p=128), gst.t[:], reads=[gst], owner=gst)


def seqs_of(k):
    return k.seqs


def phase_B(k):
    nc, S = k.nc, k.S
    ar = Arena(nc)
    smax = k.smax
    ntm = smax // 128
    ident_f = ar.alloc("ident_f", [128, 128], F32)
    ident_b = ar.alloc("ident_b", [128, 128], BF16)
    ones_f = ar.alloc("ones_f", [128, 128], F32)
    ones_b1 = ar.alloc("ones_b1", [128, 128], BF16)
    ones_b128 = ar.alloc("ones_b128", [128, 128], BF16)
    ma2 = [ar.alloc(f"ma2{d}", [128, 256], F32) for d in range(2)]
    mbm = [ar.alloc(f"mb{d}", [128, 128], F32) for d in range(2)]
    mac = [ar.alloc(f"mac{d}", [128, 128], F32) for d in range(2)]
    neg2 = [ar.alloc(f"neg{d}", [128, 256], BF16) for d in range(2)]
    convw = ar.alloc("convw", [128, 12, 5], F32)
    diag = ar.alloc("diag", [128, 60, 128], BF16)
    negA = ar.alloc("negA", [128, 8], F32)
    dtb = ar.alloc("dtb", [128, 8], F32)
    gg = ar.alloc("gg", [128, 128], F32)
    junk = ar.alloc("junk", [128, 128], F32)
    G = ar.alloc("G", [128, ntm, 16], F32)
    BT = ar.alloc("BT", [128, ntm, 8], F32)
    NBT = ar.alloc("NBT", [128, ntm, 8], F32)
    GA = ar.alloc("GA", [128, ntm, 8], F32)
    EGL = ar.alloc("EGL", [128, ntm * 8], F32)
    ssq = ar.alloc("ssq", [128, ntm], F32)
    rstd = ar.alloc("rstd", [128, ntm], F32)
    qT = ar.alloc("qT", [128, smax], BF16)
    kT = ar.alloc("kT", [128, smax], BF16)
    k_tok = ar.alloc("k_tok", [128, ntm * 128], BF16)
    v_tok = ar.alloc("v_tok", [128, ntm * 128], BF16)
    oacc = ar.alloc("oacc", [128, ntm * 128], F32)
    mark = ar.off
    raw_r = ar.ring("raw", 6, [128, 516], BF16)
    c_r = ar.ring("c", 8, [128, 512], F32)
    vb_r = ar.ring("vb", 3, [128, 512], BF16)
    sq_r = ar.ring("sq", 4, [128, 512], BF16)
    ln_r = ar.ring("ln", 5, [128, 512], F32)
    end1 = ar.off
    ar.off = mark
    NL, DEPTH = 3, 6
    lanes = []
    for l in range(NL):
        lanes.append(dict(mag=ar.alloc(f"mag{l}", [128, 256], F32), e3=ar.alloc(f"e3{l}", [128, 384], F32),
                          xy=ar.ring(f"xy{l}", 3, [128, 256], F32), R=ar.ring(f"R{l}", 3, [128, 128], F32),
                          P1=k.psum[2 * l], P2=k.psum[2 * l + 1]))
    slots = [[dict(Rb=ar.alloc("Rb", [128, 128], BF16), AT=ar.alloc("AT", [128, 128], BF16), kgT=ar.alloc("kgT", [128, 128], BF16),
                   qdT=ar.alloc("qdT", [128, 128], BF16), kd=ar.alloc("kd", [128, 128], BF16), gl=ar.alloc("gl", [128, 2], F32))
              for _ in range(DEPTH)] for d in range(2)]
    rr_r = [ar.ring(f"rr{d}", 3, [128, 128], BF16) for d in range(2)]
    vn_r = [ar.ring(f"vn{d}", 3, [128, 128], BF16) for d in range(2)]
    Sf = [ar.alloc(f"Sf{d}", [128, 128], F32) for d in range(2)]
    Sb = [ar.alloc(f"Sb{d}", [128, 128], BF16) for d in range(2)]
    end2 = ar.off
    ar.off = end1
    z_r = ar.ring("Z", 2, [128, 16, 128], F32)
    on_r = ar.ring("on", 3, [128, 128], BF16)
    ob_r = ar.ring("ob", 2, [128, 512], BF16)
    ar.off = max(end1, end2, ar.off)
    bk = Ring(k.psum[0:6])
    pS = k.psum[6:8]

    S.dma("sp", ident_f.t[:], k.c_ident_f, writes=[ident_f], owner=ident_f)
    S.dma("sp", ident_b.t[:], k.c_ident_b, writes=[ident_b], owner=ident_b)
    for d in range(2):
        S.dma("sp", ma2[d].t[:], k.c_gm_f[d], writes=[ma2[d]], owner=ma2[d])
        S.dma("sp", mbm[d].t[:], k.c_gm_mb[d], writes=[mbm[d]], owner=mbm[d])
        S.dma("sp", mac[d].t[:], k.c_gm_mb[2 + d], writes=[mac[d]], owner=mac[d])
        S.dma("sp", neg2[d].t[:], k.c_gm_neg[d], writes=[neg2[d]], owner=neg2[d])
    S.memset("pool", ones_f.t[:], 1.0, writes=[ones_f])
    S.memset("pool", ones_b1.t[:], 1.0, writes=[ones_b1])
    S.memset("pool", ones_b128.t[:], 128.0, writes=[ones_b128])
    cw_v = k.conv_w.rearrange("j (c p) -> c p j", p=128)
    for c in range(12):
        S.op("sp", "dma_start", dict(out=convw.t[:, c, :], in_=cw_v[c], allow_slow_non_contiguous=True), (), [convw.b], dma_owner=convw.b)
    for c in range(12):
        for j in range(5):
            S.ts("dve", diag.t[:, c * 5 + j, :], ident_f.t[:], convw.t[:, c, j:j + 1], None, ALU.mult,
                 reads=[ident_f, convw], writes=[diag])
    S.dma("sp", negA.t[:], k.a_log[0:1, :].broadcast_to([128, 8]), writes=[negA], owner=negA)
    S.dma("sp", dtb.t[:], k.dt_bias[0:1, :].broadcast_to([128, 8]), writes=[dtb], owner=dtb)
    S.dma("sp", gg.t[:], k.gdn_g[0:1, :].broadcast_to([128, 128]), writes=[gg], owner=gg)
    S.act(negA.t[:], negA.t[:], AF.Exp, reads=[negA], writes=[negA])
    S.ts("dve", negA.t[:], negA.t[:], -1.0, None, ALU.mult, reads=[negA], writes=[negA])
    S.ts("dve", gg.t[:], gg.t[:], math.sqrt(128.0), None, ALU.mult, reads=[gg], writes=[gg])

    def bview(bank):
        return bank.t[:, :].bitcast(BF16)

    ev = [0]

    def evac(out_ap, in_ap, reads, writes):
        S.cp("act" if ev[0] % 2 == 0 else "dve", out_ap, in_ap, reads, writes)
        ev[0] += 1

    for (s0, sl) in seq_starts(k):
        nt = sl // 128
        for c0 in range(0, nt, 16):
            c1 = min(nt, c0 + 16)
            S.dma("sp", G.t[:, c0:c1, :], k.gs[s0 + c0 * 128:s0 + c1 * 128, :].rearrange("(n p) c -> p n c", p=128), writes=[G], owner=G)
        S.act(BT.t[:, 0:nt, :], G.t[:, 0:nt, 0:8], AF.Exp, reads=[G], writes=[BT], scale=-1.0)
        S.ts("dve", BT.t[:, 0:nt, :], BT.t[:, 0:nt, :], 1.0, None, ALU.add, reads=[BT], writes=[BT])
        S.op("dve", "reciprocal", dict(out=BT.t[:, 0:nt, :], in_=BT.t[:, 0:nt, :]), reads=[BT], writes=[BT])
        S.ts("dve", NBT.t[:, 0:nt, :], BT.t[:, 0:nt, :], -1.0, None, ALU.mult, reads=[BT], writes=[NBT])
        S.tt("dve", GA.t[:, 0:nt, :], G.t[:, 0:nt, 8:16], dtb.t[:, 0:8].unsqueeze(1).broadcast_to([128, nt, 8]), ALU.add, reads=[G, dtb], writes=[GA])
        S.act(GA.t[:, 0:nt, :], GA.t[:, 0:nt, :], AF.Exp, reads=[GA], writes=[GA])
        S.act(GA.t[:, 0:nt, :], GA.t[:, 0:nt, :], AF.Ln, reads=[GA], writes=[GA], bias=1.0)
        S.tt("dve", GA.t[:, 0:nt, :], GA.t[:, 0:nt, :], negA.t[:, 0:8].unsqueeze(1).broadcast_to([128, nt, 8]), ALU.mult, reads=[GA, negA], writes=[GA])
        bank = bk.next()
        for t in range(nt):
            for d in range(2):
                S.mm(bank.t[:, t * 8 + d * 4:t * 8 + d * 4 + 4], mac[d].t[:], GA.t[:, t, d * 4:(d + 1) * 4], reads=[mac[d], GA], writes=[bank])
        S.act(EGL.t[:, 0:nt * 8], bank.t[:, 0:nt * 8], AF.Exp, reads=[bank], writes=[EGL])

        for h in range(4):
            nblk = sl // 512

            def b1_gen(blk):
                t0 = s0 + blk * 512
                cs, banks_c = [], []
                for which in range(3):
                    cidx = which * 4 + h
                    raw = raw_r.next()
                    lo, hi = 0, 516
                    if blk == 0:
                        S.memset("pool", raw.t[:, 0:2], 0.0, writes=[raw])
                        lo = 2
                    if blk == nblk - 1:
                        S.memset("pool", raw.t[:, 514:516], 0.0, writes=[raw])
                        hi = 514
                    S.dma("sp", raw.t[:, lo:hi], k.qkvT[cidx * 128:(cidx + 1) * 128, t0 - 2 + lo:t0 - 2 + hi], writes=[raw], owner=raw)
                    bank = bk.next()
                    for j in range(5):
                        S.mm(bank.t[:, :], diag.t[:, cidx * 5 + j, :], raw.t[:, j:j + 512], start=(j == 0), stop=(j == 4), reads=[diag, raw], writes=[bank])
                    banks_c.append(bank)
                yield
                for which in range(3):
                    c = c_r.next()
                    S.act(c.t[:], banks_c[which].t[:, :], AF.Silu, reads=[banks_c[which]], writes=[c])
                    cs.append(c)
                yield
                vb = vb_r.next()
                S.cp("pool", vb.t[:], cs[2].t[:], reads=[cs[2]], writes=[vb])
                sqs = []
                for which in range(2):
                    sq = sq_r.next()
                    S.tt("pool", sq.t[:], cs[which].t[:], cs[which].t[:], ALU.mult, reads=[cs[which]], writes=[sq])
                    sqs.append(sq)
                yield
                bks = []
                for which in range(2):
                    bank = bk.next()
                    S.mm(bank.t[:, :], (ones_b128 if which == 0 else ones_b1).t[:], sqs[which].t[:], reads=[ones_b128, ones_b1, sqs[which]], writes=[bank])
                    bks.append(bank)
                yield
                rss = []
                for which in range(2):
                    ln = ln_r.next()
                    S.act(ln.t[:], bks[which].t[:, :], AF.Ln, reads=[bks[which]], writes=[ln], bias=(128.0e-6 if which == 0 else 1.0e-6))
                    rss.append(ln)
                yield
                for which in range(2):
                    S.act(rss[which].t[:], rss[which].t[:], AF.Exp, reads=[rss[which]], writes=[rss[which]], scale=-0.5)
                yield
                for which in range(2):
                    dst = qT if which == 0 else kT
                    S.tt("dve", dst.t[:, blk * 512:(blk + 1) * 512], cs[which].t[:], rss[which].t[:], ALU.mult, reads=[cs[which], rss[which]], writes=[dst])
                yield
                trb = k.psum[6 + blk % 2]
                tv = bview(trb)
                for a in range(4):
                    S.tr(tv[:, a * 128:(a + 1) * 128], kT.t[:, blk * 512 + a * 128:blk * 512 + (a + 1) * 128], ident_b.t[:], reads=[kT, ident_b], writes=[trb])
                for a in range(4):
                    S.tr(tv[:, 512 + a * 128:512 + (a + 1) * 128], vb.t[:, a * 128:(a + 1) * 128], ident_b.t[:], reads=[vb, ident_b], writes=[trb])
                yield
                evac(k_tok.t[:, blk * 512:(blk + 1) * 512], tv[:, 0:512], [trb], [k_tok])
                evac(v_tok.t[:, blk * 512:(blk + 1) * 512], tv[:, 512:1024], [trb], [v_tok])

            gens, nb = [], 0
            while nb < nblk or gens:
                while len(gens) < 2 and nb < nblk:
                    gens.append(b1_gen(nb))
                    nb += 1
                for g in list(gens):
                    try:
                        next(g)
                    except StopIteration:
                        gens.remove(g)

            S.barrier()
            for d in range(2):
                S.memset("dve", Sf[d].t[:], 0.0, writes=[Sf[d]])
                S.memset("pool", Sb[d].t[:], 0.0, writes=[Sb[d]])
            prep_done = set()
            oacc_written = set()
            scan_emitted = [0, 0]

            def prep_gen(i, d, ln):
                tile = i if d == 0 else nt - 1 - i
                gcol = d * 4 + h
                tc = slice(tile * 128, (tile + 1) * 128)
                sl_ = slots[d][i % DEPTH]
                mag, e3, P1, P2 = ln["mag"], ln["e3"], ln["P1"], ln["P2"]
                S.ts("dve", mag.t[:], ma2[d].t[:], GA.t[:, tile, gcol:gcol + 1], None, ALU.mult, reads=[ma2[d], GA], writes=[mag])
                S.mm(P1.t[:, 0:256], mbm[d].t[:], mag.t[:], start=True, stop=False, reads=[mbm[d], mag], writes=[P1])
                S.mm(P1.t[:, 0:256], ident_b.t[:], neg2[d].t[:], start=False, stop=True, reads=[ident_b, neg2[d]], writes=[P1])
                S.mm(P1.t[:, 256:384], ones_f.t[:], mag.t[:, 0:128], reads=[ones_f, mag], writes=[P1])
                S.mm(P2.t[:, 0:128], kT.t[:, tc], kT.t[:, tc], reads=[kT], writes=[P2])
                S.mm(P2.t[:, 128:256], kT.t[:, tc], qT.t[:, tc], reads=[kT, qT], writes=[P2])
                yield
                S.act(e3.t[:], P1.t[:, 0:384], AF.Exp, reads=[P1], writes=[e3])
                yield
                xy = ln["xy"].next()
                S.stt(xy.t[:, 0:128], P2.t[:, 0:128], NBT.t[:, tile, gcol:gcol + 1], e3.t[:, 0:128], ALU.mult, ALU.mult, reads=[P2, NBT, e3], writes=[xy])
                S.tt("dve", sl_["AT"].t[:], P2.t[:, 128:256], e3.t[:, 128:256], ALU.mult, reads=[P2, e3], writes=[sl_["AT"]])
                yield
                S.tr(P1.t[:, 384:512], xy.t[:, 0:128], ident_f.t[:], reads=[xy, ident_f], writes=[P1])
                S.tt("pool", sl_["kgT"].t[:], kT.t[:, tc], e3.t[:, 256:384], ALU.mult, reads=[kT, e3], writes=[sl_["kgT"]])
                S.tt("pool", sl_["qdT"].t[:], qT.t[:, tc], e3.t[:, 256:384], ALU.mult, reads=[qT, e3], writes=[sl_["qdT"]])
                S.ts("dve", sl_["kd"].t[:], k_tok.t[:, tc], EGL.t[:, tile * 8 + gcol:tile * 8 + gcol + 1], None, ALU.mult, reads=[k_tok, EGL], writes=[sl_["kd"]])
                for cc in range(2):
                    glc = 256 + (cc * 64 + 63 if d == 0 else cc * 64)
                    S.cp("pool", sl_["gl"].t[:, cc:cc + 1], e3.t[:, glc:glc + 1], reads=[e3], writes=[sl_["gl"]])
                yield
                S.cp("act", xy.t[:, 128:256], P1.t[:, 384:512], reads=[P1], writes=[xy])
                R = ln["R"].next()
                S.tt("dve", R.t[:], xy.t[:, 0:128], ident_f.t[:], ALU.add, reads=[xy, ident_f], writes=[R])
                yield
                for m in range(1, 6):
                    last = (m == 5)
                    if not last:
                        S.mm(P2.t[:, 0:128], xy.t[:, 128:256], xy.t[:, 0:128], reads=[xy], writes=[P2])
                    S.mm(P2.t[:, 128:256], xy.t[:, 0:128], xy.t[:, 128:256], reads=[xy], writes=[P2])
                    yield
                    xyn = ln["xy"].next()
                    c0 = 128 if last else 0
                    S.cp("act", xyn.t[:, c0:256], P2.t[:, c0:256], reads=[P2], writes=[xyn])
                    yield
                    S.mm(P2.t[:, 256:384], xyn.t[:, 128:256], R.t[:], reads=[xyn, R], writes=[P2])
                    yield
                    Rn = sl_["Rb"] if last else ln["R"].next()
                    S.tt("dve", Rn.t[:], R.t[:], P2.t[:, 256:384], ALU.add, reads=[R, P2], writes=[Rn])
                    yield
                    xy, R = xyn, Rn
                prep_done.add((i, d))

            def scan_gen(d):
                P3 = pS[d]
                for i in range(nt):
                    while (i, d) not in prep_done:
                        yield
                    tile = i if d == 0 else nt - 1 - i
                    gcol = d * 4 + h
                    tc = slice(tile * 128, (tile + 1) * 128)
                    sl_ = slots[d][i % DEPTH]
                    Rb, AT, kgT, qdT, kd, gl = sl_["Rb"], sl_["AT"], sl_["kgT"], sl_["qdT"], sl_["kd"], sl_["gl"]
                    for cc in ((0, 1) if d == 0 else (1, 0)):
                        r0 = cc * 64
                        rs = slice(r0, r0 + 64)
                        S.mm(P3.t[rs, 0:128], kgT.t[:, rs], Sb[d].t[:], reads=[kgT, Sb[d]], writes=[P3])
                        yield
                        rr = rr_r[d].next()
                        S.tt("dve", rr.t[rs, :], v_tok.t[rs, tc], P3.t[rs, 0:128], ALU.subtract, reads=[v_tok, P3], writes=[rr])
                        yield
                        S.mm(P3.t[rs, 128:256], Rb.t[rs, rs], rr.t[rs, :], reads=[Rb, rr], writes=[P3])
                        yield
                        vn = vn_r[d].next()
                        S.act(vn.t[rs, :], P3.t[rs, 128:256], AF.Copy, reads=[P3, BT], writes=[vn], scale=BT.t[rs, tile, gcol:gcol + 1])
                        yield
                        S.mm(P3.t[rs, 256:384], qdT.t[:, rs], Sb[d].t[:], start=True, stop=False, reads=[qdT, Sb[d]], writes=[P3])
                        S.mm(P3.t[rs, 256:384], AT.t[rs, rs], vn.t[rs, :], start=False, stop=True, reads=[AT, vn], writes=[P3])
                        S.mm(P3.t[:, 384:512], kd.t[rs, :], vn.t[rs, :], reads=[kd, vn], writes=[P3])
                        yield
                        S.stt(Sb[d].t[:], Sf[d].t[:], gl.t[:, cc:cc + 1], P3.t[:, 384:512], ALU.mult, ALU.add, reads=[Sf[d], gl, P3], writes=[Sb[d]])
                        S.stt(Sf[d].t[:], Sf[d].t[:], gl.t[:, cc:cc + 1], P3.t[:, 384:512], ALU.mult, ALU.add, reads=[Sf[d], gl, P3], writes=[Sf[d]])
                        yield
                    if tile not in oacc_written:
                        oacc_written.add(tile)
                        S.cp("act", oacc.t[:, tc], P3.t[:, 256:384], reads=[P3], writes=[oacc])
                    else:
                        S.tt("dve", oacc.t[:, tc], oacc.t[:, tc], P3.t[:, 256:384], ALU.add, reads=[oacc, P3], writes=[oacc])
                    scan_emitted[d] = i + 1
                    yield

            plist = [(i, d) for i in range(nt) for d in range(2)]
            pnext = 0
            lane_gen = [None] * NL
            scans = [scan_gen(0), scan_gen(1)]
            alive = [True, True]
            while pnext < len(plist) or any(g is not None for g in lane_gen) or any(alive):
                for l in range(NL):
                    if lane_gen[l] is None and pnext < len(plist):
                        i_, d_ = plist[pnext]
                        if i_ < scan_emitted[d_] + DEPTH:
                            lane_gen[l] = prep_gen(i_, d_, lanes[l])
                            pnext += 1
                    if lane_gen[l] is not None:
                        try:
                            next(lane_gen[l])
                        except StopIteration:
                            lane_gen[l] = None
                for d in range(2):
                    if alive[d]:
                        try:
                            next(scans[d])
                        except StopIteration:
                            alive[d] = False
            S.barrier()

            for t in range(nt):
                S.op("dve", "scalar_tensor_tensor", dict(out=junk.t[:], in0=oacc.t[:, t * 128:(t + 1) * 128], scalar=1.0, in1=oacc.t[:, t * 128:(t + 1) * 128],
                                                         op0=ALU.mult, op1=ALU.mult, accum_out=ssq.t[:, t:t + 1]), reads=[oacc], writes=[junk, ssq])
            S.act(rstd.t[:, 0:nt], ssq.t[:, 0:nt], AF.Ln, reads=[ssq], writes=[rstd], bias=128.0e-6)
            S.act(rstd.t[:, 0:nt], rstd.t[:, 0:nt], AF.Exp, reads=[rstd], writes=[rstd], scale=-0.5)
            for g0 in range(0, nt, 16):
                n = min(16, nt - g0)
                Z = z_r.next()
                S.dma("sp", Z.t[:, 0:n, :], k.zs[s0 + g0 * 128:s0 + (g0 + n) * 128, h * 128:(h + 1) * 128].rearrange("(n p) c -> p n c", p=128), writes=[Z], owner=Z)
                S.act(Z.t[:, 0:n, :], Z.t[:, 0:n, :], AF.Silu, reads=[Z], writes=[Z])
                S.tt("pool", Z.t[:, 0:n, :], Z.t[:, 0:n, :], gg.t[:, 0:128].unsqueeze(1).broadcast_to([128, n, 128]), ALU.mult, reads=[Z, gg], writes=[Z])
                for t in range(g0, g0 + n):
                    on = on_r.next()
                    S.stt(on.t[:], oacc.t[:, t * 128:(t + 1) * 128], rstd.t[:, t:t + 1], Z.t[:, t - g0, :], ALU.mult, ALU.mult, reads=[oacc, rstd, Z], writes=[on])
                    trb = k.psum[6 + (t // 4) % 2]
                    tv = bview(trb)
                    S.tr(tv[:, (t % 4) * 128:(t % 4 + 1) * 128], on.t[:], ident_b.t[:], reads=[on, ident_b], writes=[trb])
                    if t % 4 == 3:
                        ob = ob_r.next()
                        evac(ob.t[:], tv[:, 0:512], [trb], [ob])
                        S.dma("sp", k.mixT[h * 128:(h + 1) * 128, s0 + (t - 3) * 128:s0 + (t + 1) * 128], ob.t[:], reads=[ob], owner=ob)


def seq_starts(k):
    out, s0 = [], 0
    for sl in k.seqs:
        out.append((s0, sl))
        s0 += sl
    return out


def phase_C(k):
    nc, S = k.nc, k.S
    ar = Arena(nc)
    smax = k.smax
    ntmax = smax // 128
    ident_b = ar.alloc("ident_b", [128, 128], BF16)
    lq = ar.alloc("lq", [128, 256], F32)
    pr = ar.alloc("pr", [128, 128], F32)
    sm = ar.alloc("sm", [128, 8], F32)
    nlam = ar.alloc("nlam", [128, 1], F32)
    gt = ar.alloc("gt", [128, 128], F32)
    qT_r = ar.ring("qT", 2, [128, smax], BF16)
    kT_r = ar.ring("kT", 2, [128, smax], BF16)
    va_r = ar.ring("va", 2, [128, ntmax, 129], BF16)
    pT_r = ar.ring("pT", 3, [128, 1024], BF16)
    accs_r = ar.ring("accs", 2, [128, 3, 387], F32)
    fin_r = ar.ring("fin", 8, [128, 8], F32)
    t_r = ar.ring("tt", 4, [128, 128], F32)
    o_r = ar.ring("oo", 8, [128, 128], F32)
    junk = ar.alloc("junk", [128, 128], F32)
    on_r = ar.ring("on", 4, [128, 128], BF16)
    ob_r = ar.ring("obT", 2, [128, 512], BF16)
    trb = k.psum[7]
    acc_banks = k.psum[4:7]

    S.dma("sp", ident_b.t[:], k.c_ident_b, writes=[ident_b], owner=ident_b)
    S.dma("sp", lq.t[:], k.lam_qk[0:1, :].broadcast_to([128, 256]), writes=[lq], owner=lq)
    S.dma("sp", gt.t[:], k.diff_g[0:1, :].broadcast_to([128, 128]), writes=[gt], owner=gt)
    S.tt("dve", pr.t[:, 0:64], lq.t[:, 0:64], lq.t[:, 64:128], ALU.mult, reads=[lq], writes=[pr])
    S.tt("dve", pr.t[:, 64:128], lq.t[:, 128:192], lq.t[:, 192:256], ALU.mult, reads=[lq], writes=[pr])
    S.op("dve", "reduce_sum", dict(out=sm.t[:, 0:1], in_=pr.t[:, 0:64], axis=AX.X), reads=[pr], writes=[sm])
    S.op("dve", "reduce_sum", dict(out=sm.t[:, 1:2], in_=pr.t[:, 64:128], axis=AX.X), reads=[pr], writes=[sm])
    S.act(sm.t[:, 2:4], sm.t[:, 0:2], AF.Exp, reads=[sm], writes=[sm])
    S.tt("dve", sm.t[:, 4:5], sm.t[:, 3:4], sm.t[:, 2:3], ALU.subtract, reads=[sm], writes=[sm])
    S.ts("dve", nlam.t[:], sm.t[:, 4:5], -LAM_INIT, None, ALU.add, reads=[sm], writes=[nlam])
    S.ts("dve", gt.t[:], gt.t[:], (1.0 - LAM_INIT) * math.sqrt(128.0), None, ALU.mult, reads=[gt], writes=[gt])
    for va in va_r.items:
        S.memset("pool", va.t[:, :, 128:129], 1.0, writes=[va])

    jobs = [(s0, sl, h) for (s0, sl) in seq_starts(k) for h in range(4)]

    def load(job):
        s0, sl, h = job
        qT, kT, va = qT_r.next(), kT_r.next(), va_r.next()
        S.dma("sp", qT.t[:, 0:sl], k.dqT[h * 128:(h + 1) * 128, s0:s0 + sl], writes=[qT], owner=qT)
        S.dma("sp", kT.t[:, 0:sl], k.dkT[h * 128:(h + 1) * 128, s0:s0 + sl], writes=[kT], owner=kT)
        nt = sl // 128
        for c0 in range(0, nt, 8):
            c1 = min(nt, c0 + 8)
            S.dma("sp", va.t[:, c0:c1, 0:128], k.dvs[s0 + c0 * 128:s0 + c1 * 128, h * 128:(h + 1) * 128].rearrange("(n p) d -> p n d", p=128),
                  writes=[va], owner=va)
        return qT, kT, va

    pending = []

    def fin_gen(accs, h, t0):
        fins, oos = [], []
        for qi in range(4):
            a0, a1 = qi, 4 + qi
            A0 = accs.t[:, a0 // 3, (a0 % 3) * 129:(a0 % 3) * 129 + 129]
            A1 = accs.t[:, a1 // 3, (a1 % 3) * 129:(a1 % 3) * 129 + 129]
            fin = fin_r.next()
            S.op("dve", "reciprocal", dict(out=fin.t[:, 0:1], in_=A0[:, 128:129]), reads=[accs], writes=[fin])
            S.op("dve", "reciprocal", dict(out=fin.t[:, 1:2], in_=A1[:, 128:129]), reads=[accs], writes=[fin])
            S.ts("dve", fin.t[:, 2:3], fin.t[:, 1:2], nlam.t[:, 0:1], None, ALU.mult, reads=[fin, nlam], writes=[fin])
            tt = t_r.next()
            S.ts("dve", tt.t[:], A1[:, 0:128], fin.t[:, 2:3], None, ALU.mult, reads=[accs, fin], writes=[tt])
            oo = o_r.next()
            S.stt(oo.t[:], A0[:, 0:128], fin.t[:, 0:1], tt.t[:], ALU.mult, ALU.add, reads=[accs, fin, tt], writes=[oo])
            S.op("dve", "scalar_tensor_tensor", dict(out=junk.t[:], in0=oo.t[:], scalar=1.0, in1=oo.t[:], op0=ALU.mult, op1=ALU.mult, accum_out=fin.t[:, 3:4]),
                 reads=[oo], writes=[junk, fin])
            fins.append(fin)
            oos.append(oo)
        yield
        for qi in range(4):
            fin = fins[qi]
            S.act(fin.t[:, 5:6], fin.t[:, 3:4], AF.Ln, reads=[fin], writes=[fin], bias=128.0 * 1e-6)
            S.act(fin.t[:, 4:5], fin.t[:, 5:6], AF.Exp, reads=[fin], writes=[fin], scale=-0.5)
        yield
        obT = ob_r.next()
        for qi in range(4):
            on = on_r.next()
            S.stt(on.t[:], oos[qi].t[:], fins[qi].t[:, 4:5], gt.t[:], ALU.mult, ALU.mult, reads=[oos[qi], fins[qi], gt], writes=[on])
            trv = trb.t[:, qi * 64:(qi + 1) * 64].bitcast(BF16)
            S.tr(trv, on.t[:], ident_b.t[:], reads=[on, ident_b], writes=[trb])
        yield
        S.cp("dve", obT.t[:], trb.t[:, 0:256].bitcast(BF16), reads=[trb], writes=[obT])
        S.dma("sp", k.mixT[512 + h * 128:512 + (h + 1) * 128, t0:t0 + 512], obT.t[:], reads=[obT], owner=obT)

    pair_i = [0]
    nxt = load(jobs[0])
    for ji, (s0, sl, h) in enumerate(jobs):
        qT, kT, va = nxt
        if ji + 1 < len(jobs):
            nxt = load(jobs[ji + 1])
        nt = sl // 128
        for qb in range(sl // 512):
            scs = {}

            def score(kt):
                p = pair_i[0] % 2
                pair_i[0] += 1
                for comp in range(2):
                    sc = k.psum[2 * p + comp]
                    S.mm(sc.t[:, :], kT.t[comp * 64:(comp + 1) * 64, kt * 128:(kt + 1) * 128], qT.t[comp * 64:(comp + 1) * 64, qb * 512:(qb + 1) * 512],
                         reads=[kT, qT], writes=[sc])
                scs[kt] = p

            started = set()
            score(0)
            for kt in range(nt):
                p = scs.pop(kt)
                pT = pT_r.next()
                S.act(pT.t[:], k.pspair[p][:, 0:1024], AF.Exp, reads=[k.psum[2 * p], k.psum[2 * p + 1]], writes=[pT])
                if kt + 1 < nt:
                    score(kt + 1)
                if pending and kt % 2 == 1:
                    try:
                        next(pending[0])
                    except StopIteration:
                        pending.pop(0)
                for comp in range(2):
                    for qi in range(4):
                        ai = comp * 4 + qi
                        bank = acc_banks[ai // 3]
                        c0 = (ai % 3) * 129
                        st = (kt == 0) and (ai // 3 not in started)
                        started.add(ai // 3)
                        S.mm(bank.t[:, c0:c0 + 129], pT.t[:, comp * 512 + qi * 128:comp * 512 + (qi + 1) * 128], va.t[:, kt, :], start=st, stop=(kt == nt - 1),
                             reads=[pT, va], writes=[bank])
            accs = accs_r.next()
            S.cp("act", accs.t[:, 0, :], acc_banks[0].t[:, 0:387], reads=[acc_banks[0]], writes=[accs])
            S.cp("dve", accs.t[:, 1, :], acc_banks[1].t[:, 0:387], reads=[acc_banks[1]], writes=[accs])
            S.cp("act", accs.t[:, 2, 0:258], acc_banks[2].t[:, 0:258], reads=[acc_banks[2]], writes=[accs])
            while pending:
                g = pending.pop(0)
                for _ in g:
                    pass
            pending.append(fin_gen(accs, h, s0 + qb * 512))
    while pending:
        g = pending.pop(0)
        for _ in g:
            pass


TD = 128


def phase_D(k):
    nc, S, T = k.nc, k.S, k.T
    ar = Arena(nc)
    w_out = ar.alloc("w_out", [128, 8, 1024], BF16)
    w_gu = ar.alloc("w_gu", [128, 8, 2 * D_FF], BF16)
    w_dn = ar.alloc("w_dn", [128, 22, 1024], BF16)
    lnp = ar.alloc("lnp", [128, 4, 1024], F32)
    ident_f = ar.alloc("ident_f", [128, 128], F32)
    mark = ar.off
    wst = ar.ring("wst", 2, [128, D_FF], F32)
    S.dma("sp", ident_f.t[:], k.c_ident_f, writes=[ident_f], owner=ident_f)
    for i, p in enumerate((k.ln1_g, k.ln1_b, k.ln2_g, k.ln2_b)):
        S.dma("sp", lnp.t[:, i, :], p[0:1, :].broadcast_to([128, 1024]), writes=[lnp], owner=lnp)
    ci = [0]

    def cast(out_ap, st, n, wt):
        S.cp(["dve", "pool", "act"][ci[0] % 3], out_ap, st.t[:, 0:n], reads=[st], writes=[wt])
        ci[0] += 1

    wo_v = k.w_out.rearrange("(kk p) c -> kk p c", p=128)
    for kk in range(8):
        st = wst.next()
        S.dma("sp", st.t[:, 0:1024], wo_v[kk], writes=[st], owner=st)
        cast(w_out.t[:, kk, :], st, 1024, w_out)
    wg_v = k.w_gu.rearrange("(kk p) c -> kk p c", p=128)
    for kk in range(8):
        for half in range(2):
            st = wst.next()
            S.dma("sp", st.t[:, :], wg_v[kk][:, half * D_FF:(half + 1) * D_FF], writes=[st], owner=st)
            cast(w_gu.t[:, kk, half * D_FF:(half + 1) * D_FF], st, D_FF, w_gu)
    wd_v = k.w_down.rearrange("(j p) c -> j p c", p=128)
    for j in range(22):
        st = wst.next()
        S.dma("sp", st.t[:, 0:1024], wd_v[j], writes=[st], owner=st)
        cast(w_dn.t[:, j, :], st, 1024, w_dn)
    S.barrier()
    ar.off = mark
    mx_r = ar.ring("mixT", 3, [128, 8, TD], BF16)
    xx_r = ar.ring("x", 2, [128, 1024], F32)
    x1_r = ar.ring("x1", 2, [128, 1024], F32)
    x1T_r = ar.ring("x1T", 2, [128, 8, TD], BF16)
    aT_r = ar.ring("aT", 2, [128, 22, TD], BF16)
    st_r = ar.ring("bst", 4, [128, 2, 6], F32)
    mv_r = ar.ring("mv", 4, [128, 4], F32)
    sl_r = ar.ring("silu", 3, [128, TD], F32)
    banks = Ring(k.psum[0:8])
    mix_v = k.mixT.rearrange("(kk p) t -> p kk t", p=128)
    ntile = T // TD
    ev = [0]

    def layer_norm(src, dst, gi):
        bst, mv = st_r.next(), mv_r.next()
        for hf in range(2):
            S.op("dve", "bn_stats", dict(out=bst.t[:, hf, :], in_=src.t[:, hf * 512:(hf + 1) * 512]), reads=[src], writes=[bst])
        S.op("dve", "bn_aggr", dict(out=mv.t[:, 0:2], in_=bst.t[:, :, :]), reads=[bst], writes=[mv])
        S.act(mv.t[:, 3:4], mv.t[:, 1:2], AF.Ln, reads=[mv], writes=[mv], bias=1e-5)
        S.act(mv.t[:, 2:3], mv.t[:, 3:4], AF.Exp, reads=[mv], writes=[mv], scale=-0.5)
        S.ts("dve", dst.t[:], src.t[:], mv.t[:, 0:1], mv.t[:, 2:3], ALU.subtract, ALU.mult, reads=[src, mv], writes=[dst])
        S.tt("pool", dst.t[:], dst.t[:], lnp.t[:, gi, :], ALU.mult, reads=[dst, lnp], writes=[dst])
        S.tt("pool", dst.t[:], dst.t[:], lnp.t[:, gi + 1, :], ALU.add, reads=[dst, lnp], writes=[dst])

    st8 = {}

    def loads(i):
        t0 = i * TD
        mx, xx = mx_r.next(), xx_r.next()
        S.dma("sp", mx.t[:], mix_v[:, :, t0:t0 + TD], writes=[mx], owner=mx)
        S.dma("sp", xx.t[:], k.x[t0:t0 + TD, :], writes=[xx], owner=xx)
        st8[i] = dict(mx=mx, xx=xx)

    def out_proj(i):
        d = st8[i]
        mx, xx = d["mx"], d["xx"]
        x1 = x1_r.next()
        d["x1"] = x1
        for hf in range(2):
            bank = banks.next()
            for kk in range(8):
                S.mm(bank.t[:, :], mx.t[:, kk, :], w_out.t[:, kk, hf * 512:(hf + 1) * 512], start=(kk == 0), stop=(kk == 7), reads=[mx, w_out], writes=[bank])
            S.stt(xx.t[:, hf * 512:(hf + 1) * 512], xx.t[:, hf * 512:(hf + 1) * 512], ALPHA, bank.t[:, :], ALU.mult, ALU.add, reads=[xx, bank], writes=[xx])
        layer_norm(xx, x1, 0)

    def transposes(i):
        d = st8[i]
        x1 = d["x1"]
        x1T = x1T_r.next()
        d["x1T"] = x1T
        for k4 in range(2):
            bank = banks.next()
            for q in range(4):
                kk = k4 * 4 + q
                S.tr(bank.t[:, q * 128:(q + 1) * 128], x1.t[:, kk * 128:(kk + 1) * 128], ident_f.t[:], reads=[x1, ident_f], writes=[bank])
            S.cp("act" if ev[0] % 2 == 0 else "dve", x1T.t[:, k4 * 4:(k4 + 1) * 4, :], bank.t[:, 0:512], reads=[bank], writes=[x1T])
            ev[0] += 1

    def gate_up(i):
        d = st8[i]
        x1T = d["x1T"]
        aT = aT_r.next()
        d["aT"] = aT
        for j in range(22):
            gb = banks.next()
            for kk in range(8):
                S.mm(gb.t[:, 0:TD], w_gu.t[:, kk, j * 128:(j + 1) * 128], x1T.t[:, kk, :], start=(kk == 0), stop=(kk == 7), reads=[w_gu, x1T], writes=[gb])
            for kk in range(8):
                S.mm(gb.t[:, 256:256 + TD], w_gu.t[:, kk, D_FF + j * 128:D_FF + (j + 1) * 128], x1T.t[:, kk, :], start=False, stop=(kk == 7), reads=[w_gu, x1T], writes=[gb])
            sl = sl_r.next()
            S.act(sl.t[:], gb.t[:, 0:TD], AF.Silu, reads=[gb], writes=[sl])
            S.tt("dve", aT.t[:, j, :], sl.t[:], gb.t[:, 256:256 + TD], ALU.mult, reads=[sl, gb], writes=[aT])

    def down(i):
        d = st8.pop(i)
        x1, aT = d["x1"], d["aT"]
        t0 = i * TD
        for hf in range(2):
            bank = banks.next()
            for j in range(22):
                S.mm(bank.t[:, :], aT.t[:, j, :], w_dn.t[:, j, hf * 512:(hf + 1) * 512], start=(j == 0), stop=(j == 21), reads=[aT, w_dn], writes=[bank])
            S.stt(x1.t[:, hf * 512:(hf + 1) * 512], x1.t[:, hf * 512:(hf + 1) * 512], ALPHA, bank.t[:, :], ALU.mult, ALU.add, reads=[x1, bank], writes=[x1])
        layer_norm(x1, x1, 2)
        S.dma("sp", k.y[t0:t0 + TD, :], x1.t[:], reads=[x1], owner=x1)

    loads(0)
    if ntile > 1:
        loads(1)
    out_proj(0)
    transposes(0)
    for i in range(ntile):
        gate_up(i)
        if i + 2 < ntile:
            loads(i + 2)
        if i + 1 < ntile:
            out_proj(i + 1)
        down(i)
        if i + 1 < ntile:
            transposes(i + 1)


_CACHE = {}


def _get_program(seqs, debug=False, phases="ABCD"):
    key = (tuple(seqs), debug, phases)
    if key not in _CACHE:
        _CACHE[key] = build(list(seqs), debug, phases)
    return _CACHE[key]


def make_in_maps(seqs, n_cores, x_prompt, x_sample, w):
    smax = max(seqs)
    c = _consts(smax)
    per = len(seqs) - 1
    shared = {
        "w_in": np.ascontiguousarray(w["w_in"][0]), "conv_w": np.ascontiguousarray(w["conv_w"][0]),
        "a_log": np.ascontiguousarray(w["a_log"][0]).reshape(1, 8), "dt_bias": np.ascontiguousarray(w["dt_bias"][0]).reshape(1, 8),
        "gdn_norm_g": np.ascontiguousarray(w["gdn_norm_g"]).reshape(1, 128), "lam_qk": np.ascontiguousarray(w["lam_qk"][0]).reshape(1, 256),
        "diff_norm_g": np.ascontiguousarray(w["diff_norm_g"]).reshape(1, 128), "w_out": np.ascontiguousarray(w["w_out"][0]),
        "ln1_g": np.ascontiguousarray(w["ln1_g"]).reshape(1, 1024), "ln1_b": np.ascontiguousarray(w["ln1_b"]).reshape(1, 1024),
        "w_gate_up": np.ascontiguousarray(w["w_gate_up"][0]), "w_down": np.ascontiguousarray(w["w_down"][0]),
        "ln2_g": np.ascontiguousarray(w["ln2_g"]).reshape(1, 1024), "ln2_b": np.ascontiguousarray(w["ln2_b"]).reshape(1, 1024),
        "c_ident_f": c["ident_f"], "c_ident_b": c["ident_b"], "c_rope": c["rope"], "c_perm": c["perm"],
        "c_gm_f": c["gm_f"], "c_gm_mb": c["gm_mb"], "c_gm_neg": c["gm_neg"],
    }
    in_maps = []
    for i in range(n_cores):
        parts = [x_prompt[i]] + [x_sample[per * i + j] for j in range(per)]
        xc = np.ascontiguousarray(np.concatenate(parts, 0))
        m = dict(shared)
        m["x"] = xc
        in_maps.append(m)
    return in_maps


def kernel(x_prompt, x_sample, w_in, conv_w, a_log, dt_bias, gdn_norm_g, lam_qk, diff_norm_g,
           w_out, ln1_g, ln1_b, w_gate_up, w_down, ln2_g, ln2_b):
    n = 8
    x_prompt = np.asarray(x_prompt, np.float32)
    x_sample = np.asarray(x_sample, np.float32)
    w = dict(w_in=w_in, conv_w=conv_w, a_log=a_log, dt_bias=dt_bias, gdn_norm_g=gdn_norm_g, lam_qk=lam_qk,
             diff_norm_g=diff_norm_g, w_out=w_out, ln1_g=ln1_g, ln1_b=ln1_b, w_gate_up=w_gate_up, w_down=w_down,
             ln2_g=ln2_g, ln2_b=ln2_b)
    w = {kk: np.asarray(v, np.float32) for kk, v in w.items()}
    Sp = x_prompt.shape[1]
    Ss = x_sample.shape[1]
    per = x_sample.shape[0] // n
    seqs = [Sp] + [Ss] * per
    nc, kk = _get_program(seqs)
    in_maps = make_in_maps(seqs, n, x_prompt, x_sample, w)
    res = run_bass_kernel_spmd(nc, in_maps, core_ids=list(range(n)))
    yp = np.empty_like(x_prompt)
    ys = np.empty_like(x_sample)
    for i in range(n):
        y = res.results[i]["y"]
        yp[i] = y[:Sp]
        for j in range(per):
            ys[per * i + j] = y[Sp + j * Ss: Sp + (j + 1) * Ss]
    return yp, ys
```
